# Optimizing a Trainium2 kernel written in Bass

```python
import jax, jax.numpy as jnp
from jax import lax
import numpy as np

D_MODEL = 2048
BATCH = 4
SEQ = 4096
DEPTH = 2

CHUNK = 64
D_FF = 5632
SSM_EXPAND = 2
D_INNER = SSM_EXPAND * D_MODEL
SSM_HEAD_DIM = 64
SSM_HEADS = D_INNER // SSM_HEAD_DIM
SSM_GROUPS = 8
SSM_STATE = 128
SSM_CONV = 4
SSM_CONV_DIM = D_INNER + 2 * SSM_GROUPS * SSM_STATE
CONF_CHANNELS = D_MODEL
CONF_KERNEL = 31
N_BRANCHES = 2
IN_SPLIT_SIZES = (D_INNER, SSM_CONV_DIM, SSM_HEADS, 2 * CONF_CHANNELS, N_BRANCHES * D_MODEL)
IN_COLS = sum(IN_SPLIT_SIZES)
IN_SPLIT_POINTS = tuple(int(v) for v in np.cumsum(IN_SPLIT_SIZES)[:-1])
N_ADA = 9
EPS = 1e-6

kernel_name = "hybrid_ssd_conformer_macaron_adaln"


def rmsnorm(x, g):
    x32 = x.astype(jnp.float32)
    y = x32 * lax.rsqrt(jnp.mean(x32 * x32, axis=-1, keepdims=True) + EPS)
    return (y * g.astype(jnp.float32)).astype(x.dtype)


def layernorm(x, g, b):
    x32 = x.astype(jnp.float32)
    mu = jnp.mean(x32, axis=-1, keepdims=True)
    var = jnp.mean(jnp.square(x32 - mu), axis=-1, keepdims=True)
    y = (x32 - mu) * lax.rsqrt(var + EPS)
    return (y * g.astype(jnp.float32) + b.astype(jnp.float32)).astype(x.dtype)


def modulate(h, shift, scale):
    return h * (1.0 + scale) + shift


def causal_dwconv(u, w, b):
    k = w.shape[0]
    out = lax.conv_general_dilated(
        u, w[:, None, :].astype(u.dtype), window_strides=(1,), padding=[(k - 1, 0)],
        dimension_numbers=("NWC", "WIO", "NWC"), feature_group_count=u.shape[-1])
    return out + b


def swiglu(h, w13, w2):
    a, g = jnp.split(h @ w13, 2, axis=-1)
    return (jax.nn.silu(g) * a) @ w2


def ssd_chunked(xs, dt, a_log, b_ssm, c_ssm):
    f32 = jnp.float32
    bsz, seq = xs.shape[:2]
    n_chunks = seq // CHUNK
    hg = SSM_HEADS // SSM_GROUPS
    x = xs.astype(f32).reshape(bsz, n_chunks, CHUNK, SSM_GROUPS, hg, SSM_HEAD_DIM)
    dtc = dt.reshape(bsz, n_chunks, CHUNK, SSM_GROUPS, hg)
    bc = b_ssm.astype(f32).reshape(bsz, n_chunks, CHUNK, SSM_GROUPS, SSM_STATE)
    cc = c_ssm.astype(f32).reshape(bsz, n_chunks, CHUNK, SSM_GROUPS, SSM_STATE)
    a = -jnp.exp(a_log.astype(f32)).reshape(SSM_GROUPS, hg)
    a_cs = jnp.cumsum(dtc * a, axis=2)
    x_dt = x * dtc[..., None]
    causal = jnp.tril(jnp.ones((CHUNK, CHUNK), bool))[None, None, :, :, None, None]
    seg = a_cs[:, :, :, None] - a_cs[:, :, None, :]
    decay = jnp.exp(jnp.where(causal, seg, -jnp.inf))
    cb = jnp.einsum("bclgn,bcsgn->bclsg", cc, bc)
    y_diag = jnp.einsum("bclsg,bclsgh,bcsghp->bclghp", cb, decay, x_dt)
    to_end = jnp.exp(a_cs[:, :, -1:] - a_cs)
    states = jnp.einsum("bclgn,bclgh,bclghp->bcghpn", bc, to_end, x_dt)
    chunk_decay = jnp.exp(a_cs[:, :, -1])

    def carry_state(h, inp):
        s, d = inp
        return h * d[..., None, None] + s, h

    h0 = jnp.zeros((bsz, SSM_GROUPS, hg, SSM_HEAD_DIM, SSM_STATE), f32)
    _, prev = lax.scan(carry_state, h0, (jnp.moveaxis(states, 1, 0), jnp.moveaxis(chunk_decay, 1, 0)))
    prev = jnp.moveaxis(prev, 0, 1)
    y_off = jnp.einsum("bclgn,bcghpn,bclgh->bclghp", cc, prev, jnp.exp(a_cs))
    return (y_diag + y_off).reshape(bsz, seq, SSM_HEADS, SSM_HEAD_DIM)


def hybrid_mixer(h, w_in, ssm_conv_w, ssm_conv_b, dt_bias, a_log, d_skip, ssm_norm_w,
                 w_ssm_out, dw_w, dw_b, conv_ln_g, conv_ln_b, w_pw2, b_pw2, w_o):
    bsz, seq, _ = h.shape
    proj = h @ w_in
    z, xbc, dt_raw, glu_in, gate_in = jnp.split(proj, IN_SPLIT_POINTS, axis=-1)
    xbc = jax.nn.silu(causal_dwconv(xbc, ssm_conv_w, ssm_conv_b))
    xs, b_ssm, c_ssm = jnp.split(xbc, (D_INNER, D_INNER + SSM_GROUPS * SSM_STATE), axis=-1)
    xs = xs.reshape(bsz, seq, SSM_HEADS, SSM_HEAD_DIM)
    b_ssm = b_ssm.reshape(bsz, seq, SSM_GROUPS, SSM_STATE)
    c_ssm = c_ssm.reshape(bsz, seq, SSM_GROUPS, SSM_STATE)
    dt = jax.nn.softplus((dt_raw + dt_bias).astype(jnp.float32))
    y = ssd_chunked(xs, dt, a_log, b_ssm, c_ssm)
    y = y + d_skip.astype(jnp.float32)[:, None] * xs.astype(jnp.float32)
    y = y.reshape(bsz, seq, D_INNER).astype(h.dtype)
    y = rmsnorm(y * jax.nn.silu(z), ssm_norm_w)
    y_ssm = y @ w_ssm_out
    u_a, u_g = jnp.split(glu_in, 2, axis=-1)
    u = u_a * jax.nn.sigmoid(u_g)
    u = causal_dwconv(u, dw_w, dw_b)
    u = jax.nn.silu(layernorm(u, conv_ln_g, conv_ln_b))
    y_conv = u @ w_pw2 + b_pw2
    g_ssm, g_conv = jnp.split(jax.nn.sigmoid(gate_in), N_BRANCHES, axis=-1)
    return (g_ssm * y_ssm + g_conv * y_conv) @ w_o


def setup_inputs(seed: int = 0) -> dict:
    key = jax.random.key(seed)
    ks = jax.random.split(key, 32)
    f32 = jnp.float32

    def nrm(k, shape, scale):
        return jax.random.normal(k, shape, f32) * scale

    def gain(k, shape):
        return 1.0 + nrm(k, shape, 0.02)

    dt0 = jnp.exp(jax.random.uniform(ks[10], (DEPTH, SSM_HEADS), f32) * (np.log(0.1) - np.log(0.001)) + np.log(0.001))
    return {
        "x": nrm(ks[0], (BATCH, SEQ, D_MODEL), 1.0),
        "c": nrm(ks[1], (BATCH, D_MODEL), 1.0),
        "w_ada": nrm(ks[2], (DEPTH, D_MODEL, N_ADA * D_MODEL), 0.5 * D_MODEL ** -0.5),
        "b_ada": nrm(ks[3], (DEPTH, N_ADA * D_MODEL), 0.02),
        "norm_ffn1": gain(ks[4], (DEPTH, D_MODEL)),
        "ffn1_w13": nrm(ks[5], (DEPTH, D_MODEL, 2 * D_FF), D_MODEL ** -0.5),
        "ffn1_w2": nrm(ks[6], (DEPTH, D_FF, D_MODEL), D_FF ** -0.5),
        "norm_mix": gain(ks[7], (DEPTH, D_MODEL)),
        "w_in": nrm(ks[8], (DEPTH, D_MODEL, IN_COLS), D_MODEL ** -0.5),
        "ssm_conv_w": nrm(ks[9], (DEPTH, SSM_CONV, SSM_CONV_DIM), SSM_CONV ** -0.5),
        "ssm_conv_b": nrm(ks[11], (DEPTH, SSM_CONV_DIM), 0.02),
        "dt_bias": dt0 + jnp.log(-jnp.expm1(-dt0)),
        "a_log": jnp.log(jax.random.uniform(ks[12], (DEPTH, SSM_HEADS), f32, 1.0, 16.0)),
        "d_skip": 1.0 + nrm(ks[13], (DEPTH, SSM_HEADS), 0.1),
        "ssm_norm_w": gain(ks[14], (DEPTH, D_INNER)),
        "w_ssm_out": nrm(ks[15], (DEPTH, D_INNER, D_MODEL), D_INNER ** -0.5),
        "dw_w": nrm(ks[16], (DEPTH, CONF_KERNEL, CONF_CHANNELS), CONF_KERNEL ** -0.5),
        "dw_b": nrm(ks[17], (DEPTH, CONF_CHANNELS), 0.02),
        "conv_ln_g": gain(ks[18], (DEPTH, CONF_CHANNELS)),
        "conv_ln_b": nrm(ks[19], (DEPTH, CONF_CHANNELS), 0.02),
        "w_pw2": nrm(ks[20], (DEPTH, CONF_CHANNELS, D_MODEL), CONF_CHANNELS ** -0.5),
        "b_pw2": nrm(ks[21], (DEPTH, D_MODEL), 0.02),
        "w_o": nrm(ks[22], (DEPTH, D_MODEL, D_MODEL), D_MODEL ** -0.5),
        "norm_ffn2": gain(ks[23], (DEPTH, D_MODEL)),
        "ffn2_w13": nrm(ks[24], (DEPTH, D_MODEL, 2 * D_FF), D_MODEL ** -0.5),
        "ffn2_w2": nrm(ks[25], (DEPTH, D_FF, D_MODEL), D_FF ** -0.5),
        "final_norm": gain(ks[26], (D_MODEL,)),
    }


def reference(x, c, w_ada, b_ada, norm_ffn1, ffn1_w13, ffn1_w2, norm_mix, w_in,
              ssm_conv_w, ssm_conv_b, dt_bias, a_log, d_skip, ssm_norm_w, w_ssm_out,
              dw_w, dw_b, conv_ln_g, conv_ln_b, w_pw2, b_pw2, w_o,
              norm_ffn2, ffn2_w13, ffn2_w2, final_norm):
    c_act = jax.nn.silu(c)
    for l in range(DEPTH):
        ada = (c_act @ w_ada[l] + b_ada[l])[:, None, :]
        sh1, sc1, g1, sh2, sc2, g2, sh3, sc3, g3 = jnp.split(ada, N_ADA, axis=-1)
        h = modulate(rmsnorm(x, norm_ffn1[l]), sh1, sc1)
        x = x + 0.5 * g1 * swiglu(h, ffn1_w13[l], ffn1_w2[l])
        h = modulate(rmsnorm(x, norm_mix[l]), sh2, sc2)
        x = x + g2 * hybrid_mixer(h, w_in[l], ssm_conv_w[l], ssm_conv_b[l], dt_bias[l], a_log[l],
                                  d_skip[l], ssm_norm_w[l], w_ssm_out[l], dw_w[l], dw_b[l],
                                  conv_ln_g[l], conv_ln_b[l], w_pw2[l], b_pw2[l], w_o[l])
        h = modulate(rmsnorm(x, norm_ffn2[l]), sh3, sc3)
        x = x + 0.5 * g3 * swiglu(h, ffn2_w13[l], ffn2_w2[l])
    return rmsnorm(x, final_norm)
```

```python
import numpy as np
import concourse.bass as bass
import concourse.mybir as mybir
from concourse.bass_utils import run_bass_kernel_spmd

_uid = [0]
def _uniq(name):
    _uid[0] += 1
    return "%s_%d" % (name, _uid[0])


AF = mybir.ActivationFunctionType
ALU = mybir.AluOpType
F32 = mybir.dt.float32
BF16 = mybir.dt.bfloat16


class Buf:
    __slots__ = ("name", "w", "r", "psum", "sem", "sem_val", "sid")

    def __init__(self, name, psum=False):
        self.name = name
        self.sem = None
        self.sem_val = 0
        self.sid = None
        self.psum = psum
        self.w = None
        self.r = []


class Sched:
    ENGS = ("pe", "act", "dve", "pool", "sp")

    def __init__(self, nc, n_dma_sems=0):
        self.nc = nc
        self.eng = {"pe": nc.tensor, "act": nc.scalar, "dve": nc.vector, "pool": nc.gpsimd, "sp": nc.sync}
        self._ctx = []
        self.prog = {}
        for e in self.ENGS:
            cm = nc.semaphore("prog_" + e)
            self.prog[e] = cm.__enter__(); self._ctx.append(cm)
        self.seq = {e: 0 for e in self.ENGS}
        self.waited = {e: {} for e in self.ENGS}
        self.dnext = 0
        self.n_wait = 0
        self.owners = []
        self.sem_pool = []
        self.n_sems = 0

    def close(self):
        for cm in reversed(self._ctx):
            cm.__exit__(None, None, None)

    def _wait(self, e, ticket):
        if ticket is None:
            return
        if ticket[0] == "eng":
            key, sem, val = ("e", ticket[1]), self.prog[ticket[1]], ticket[2]
        else:
            key, sem, val = ("d", ticket[1].sid), ticket[1].sem, ticket[2]
        if self.waited[e].get(key, 0) >= val:
            return
        self.eng[e].wait_ge(sem, val)
        self.waited[e][key] = val
        self.n_wait += 1

    def _deps(self, e, reads, writes):
        for b in reads:
            self._wait(e, b.w)
            if b.psum:
                for t in b.r:
                    if not (t[0] == "eng" and t[1] == e):
                        self._wait(e, t)
        for b in writes:
            self._wait(e, b.w)
            for t in b.r:
                self._wait(e, t)

    def _commit(self, ticket, reads, writes):
        for b in reads:
            b.r.append(ticket)
            if len(b.r) > 24:
                best = {}
                for t in b.r:
                    k = (t[0], t[1] if t[0] == "eng" else t[1].sid)
                    if k not in best or best[k][2] < t[2]:
                        best[k] = t
                b.r = list(best.values())
        for b in writes:
            b.w = ticket
            b.r = []

    def op(self, e, fn, reads=(), writes=()):
        self._deps(e, reads, writes)
        ins = fn(self.eng[e])
        self.seq[e] += 1
        ins.then_inc(self.prog[e], 1)
        self._commit(("eng", e, self.seq[e]), reads, writes)
        return ins

    def mm_group(self, fns, reads=(), writes=()):
        self._deps("pe", reads, writes)
        ins = None
        for fn in fns:
            ins = fn(self.eng["pe"])
        self.seq["pe"] += 1
        ins.then_inc(self.prog["pe"], 1)
        self._commit(("eng", "pe", self.seq["pe"]), reads, writes)

    def dma(self, q, out, in_, reads=(), writes=(), **kw):
        owner = reads[0] if reads else writes[0]
        if owner.sem is None:
            if self.sem_pool:
                owner.sem, owner.sem_val, owner.sid = self.sem_pool.pop()
            else:
                cm = self.nc.semaphore("dq%d" % self.n_sems)
                owner.sem = cm.__enter__(); self._ctx.append(cm)
                owner.sem_val = 0; owner.sid = self.n_sems; self.n_sems += 1
            self.owners.append(owner)
        self._deps(q, reads, writes)
        if owner.sem_val:
            self._wait(q, ("dma", owner, owner.sem_val))
        owner.sem_val += 16
        self.eng[q].dma_start(out=out, in_=in_, **kw).then_inc(owner.sem, 16)
        self._commit(("dma", owner, owner.sem_val), reads, writes)

    def drain(self, e):
        for o in self.ENGS:
            if self.seq[o]:
                self._wait(e, ("eng", o, self.seq[o]))
        for b in self.owners:
            if b.sem_val:
                self._wait(e, ("dma", b, b.sem_val))

    def barrier(self):
        for e in self.ENGS:
            self.drain(e)

    def phase_end(self):
        self.barrier()
        for b in self.owners:
            self.sem_pool.append((b.sem, b.sem_val, b.sid))
            b.sem = None
        self.owners = []

    def finish(self, e, bufs):
        for b in bufs:
            self._wait(e, b.w)
            for t in b.r:
                self._wait(e, t)

def make_consts():
    i = np.arange(128)
    TRI = (i[:, None] <= i[None, :]); SU = (i[:, None] > i[None, :]); MASK = (i[None, :] >= i[:, None])
    return np.ascontiguousarray(np.concatenate([TRI, SU, MASK, np.ones((128, 128)), np.eye(128)], axis=1).astype(np.float32))


P = 128


def emit_ffn(S, nc, xT_in, xT_out, w13, w2, gs, sh, hg, gs_b, sh_b, hg_b, T, TT, D, DFF, eps=1e-6, NB=2):
    KC = D // P
    HC = DFF // P
    NS = TT // 512
    assert TT % 512 == 0 and T % TT == 0 and HC % NB == 0
    w13v = w13.rearrange("(kc p) n -> p kc n", p=P)
    w2v = w2.rearrange("(hc p) n -> p hc n", p=P)
    xin = xT_in.rearrange("(kc p) t -> p kc t", p=P)
    xout = xT_out.rearrange("(kc p) t -> p kc t", p=P)
    cms = []

    def sb(name, shape, dt):
        cm = nc.sbuf_tensor(_uniq(name), shape, dt); t = cm.__enter__(); cms.append(cm); return t

    def ps(name, shape, dt=F32):
        cm = nc.psum_tensor(_uniq(name), shape, dt); t = cm.__enter__(); cms.append(cm); return t

    x_sb = sb("ffn_x", [P, KC, TT], F32);   x_b = [Buf("x%d" % k) for k in range(KC)]
    h_sb = sb("ffn_h", [P, KC, TT], BF16);  h_b = [Buf("h%d" % s) for s in range(NS)]
    hid = sb("ffn_hid", [P, HC, TT], BF16); hid_b = [[Buf("hid") for _ in range(NS)] for _ in range(HC)]
    sq = [sb("ffn_sq%d" % i, [P, 512], BF16) for i in range(2)]; sq_b = [Buf("sq") for _ in range(2)]
    tmp = [sb("ffn_tmp%d" % i, [P, 512], F32) for i in range(2)]; tmp_b = [Buf("tmp") for _ in range(2)]
    rstd = sb("ffn_rstd", [P, 512], F32); rstd_b = Buf("rstd")
    ones = sb("ffn_ones", [P, P], BF16); ones_b = Buf("ones")
    wa = [sb("ffn_wa%d" % i, [P, KC, NB * P], BF16) for i in range(2)]; wa_b = [Buf("wa") for _ in range(2)]
    wg = [sb("ffn_wg%d" % i, [P, KC, NB * P], BF16) for i in range(2)]; wg_b = [Buf("wg") for _ in range(2)]
    w2s = [sb("ffn_w2%d" % i, [P, HC, P], BF16) for i in range(2)]; w2_b = [Buf("w2") for _ in range(2)]
    sg = [sb("ffn_sg%d" % i, [P, 512], F32) for i in range(2)]; sg_b = [Buf("sg") for _ in range(2)]
    xo = [sb("ffn_xo%d" % i, [P, 512], F32) for i in range(2)]; xo_b = [Buf("xo") for _ in range(2)]
    ss_ps = ps("ffn_ss", [P, 512]); ss_b = Buf("ss_ps", psum=True)
    a_ps = [ps("ffn_a%d" % i, [P, 512]) for i in range(2)]; a_b = [Buf("a_ps", psum=True) for _ in range(2)]
    g_ps = [ps("ffn_g%d" % i, [P, 512]) for i in range(2)]; g_b = [Buf("g_ps", psum=True) for _ in range(2)]
    o_ps = [ps("ffn_o%d" % i, [P, 512]) for i in range(2)]; o_b = [Buf("o_ps", psum=True) for _ in range(2)]
    out_bufs = []

    S.op("pool", lambda e: e.memset(ones[:], 1.0), writes=[ones_b])
    cnt = {"sq": 0, "ag": 0, "o": 0, "w": 0, "w2": 0}

    for tt in range(T // TT):
        t0 = tt * TT
        for k in range(KC):
            S.dma("sp", x_sb[:, k, :], xin[:, k, t0:t0 + TT], writes=[x_b[k]])
        for s in range(NS):
            sl = slice(s * 512, (s + 1) * 512)
            fns = []
            for k in range(KC):
                i = cnt["sq"] % 2; cnt["sq"] += 1
                S.op("act", lambda e, i=i, k=k: e.activation(out=sq[i][:], in_=x_sb[:, k, sl], func=AF.Square),
                     reads=[x_b[k]], writes=[sq_b[i]])
                S.op("pe", lambda e, i=i, k=k: e.matmul(ss_ps[:], lhsT=ones[:], rhs=sq[i][:], start=(k == 0), stop=(k == KC - 1)),
                     reads=[ones_b, sq_b[i]], writes=[ss_b])
            S.op("act", lambda e: e.activation(out=rstd[:], in_=ss_ps[:], func=AF.Sqrt, scale=1.0 / D, bias=eps),
                 reads=[ss_b], writes=[rstd_b])
            S.op("dve", lambda e: e.reciprocal(out=rstd[:], in_=rstd[:]), reads=[rstd_b], writes=[rstd_b])
            for k in range(KC):
                i = k % 2
                S.op("dve", lambda e, i=i, k=k: e.tensor_tensor(out=tmp[i][:], in0=x_sb[:, k, sl], in1=rstd[:], op=ALU.mult),
                     reads=[x_b[k], rstd_b], writes=[tmp_b[i]])
                S.op("act", lambda e, i=i, k=k: e.activation(out=h_sb[:, k, sl], in_=tmp[i][:], func=AF.Identity,
                                                            scale=gs[:, k:k + 1], bias=sh[:, k:k + 1]),
                     reads=[tmp_b[i], gs_b, sh_b], writes=[h_b[s]])
        for jb in range(HC // NB):
            wi = cnt["w"] % 2; cnt["w"] += 1
            c0 = jb * NB * P
            S.dma("pool", wa[wi][:], w13v[:, :, c0:c0 + NB * P], writes=[wa_b[wi]])
            S.dma("pool", wg[wi][:], w13v[:, :, DFF + c0:DFF + c0 + NB * P], writes=[wg_b[wi]])
            for jj in range(NB):
                j = jb * NB + jj
                for s in range(NS):
                    sl = slice(s * 512, (s + 1) * 512)
                    pi = cnt["ag"] % 2; cnt["ag"] += 1
                    S.mm_group([lambda e, k=k, pi=pi: e.matmul(a_ps[pi][:], lhsT=wa[wi][:, k, jj * P:(jj + 1) * P], rhs=h_sb[:, k, sl],
                                                               start=(k == 0), stop=(k == KC - 1)) for k in range(KC)],
                               reads=[wa_b[wi], h_b[s]], writes=[a_b[pi]])
                    S.mm_group([lambda e, k=k, pi=pi: e.matmul(g_ps[pi][:], lhsT=wg[wi][:, k, jj * P:(jj + 1) * P], rhs=h_sb[:, k, sl],
                                                               start=(k == 0), stop=(k == KC - 1)) for k in range(KC)],
                               reads=[wg_b[wi], h_b[s]], writes=[g_b[pi]])
                    S.op("act", lambda e, pi=pi: e.activation(out=sg[pi][:], in_=g_ps[pi][:], func=AF.Silu),
                         reads=[g_b[pi]], writes=[sg_b[pi]])
                    S.op("dve", lambda e, pi=pi, j=j: e.tensor_tensor(out=hid[:, j, sl], in0=a_ps[pi][:], in1=sg[pi][:], op=ALU.mult),
                         reads=[a_b[pi], sg_b[pi]], writes=[hid_b[j][s]])
        for o in range(KC):
            wi = cnt["w2"] % 2; cnt["w2"] += 1
            S.dma("pool", w2s[wi][:], w2v[:, :, o * P:(o + 1) * P], writes=[w2_b[wi]])
            for s in range(NS):
                sl = slice(s * 512, (s + 1) * 512)
                pi = cnt["o"] % 2; cnt["o"] += 1
                S.mm_group([lambda e, j=j, pi=pi: e.matmul(o_ps[pi][:], lhsT=w2s[wi][:, j, :], rhs=hid[:, j, sl],
                                                           start=(j == 0), stop=(j == HC - 1)) for j in range(HC)],
                           reads=[w2_b[wi]] + [hid_b[j][s] for j in range(HC)], writes=[o_b[pi]])
                S.op("dve", lambda e, pi=pi, o=o: e.scalar_tensor_tensor(out=xo[pi][:], in0=o_ps[pi][:], scalar=hg[:, o:o + 1],
                                                                          in1=x_sb[:, o, sl], op0=ALU.mult, op1=ALU.add),
                     reads=[o_b[pi], x_b[o], hg_b], writes=[xo_b[pi]])
                ob = Buf("xout"); out_bufs.append(ob)
                S.dma("sp", xout[:, o, t0 + s * 512:t0 + (s + 1) * 512], xo[pi][:], reads=[xo_b[pi]], writes=[ob])
    return cms, out_bufs


def emit_ffn_s(S, nc, xT_in, xT_out, w13, w2, gs, sh, hg, gs_b, sh_b, hg_b, T, TT, D, DFF, eps=1e-6, NB=2):
    KC = D // P; HC = DFF // P; NS = TT // 512
    assert TT % 512 == 0 and T % TT == 0 and HC % NB == 0
    w13v = w13.rearrange("(kc p) n -> p kc n", p=P); w2v = w2.rearrange("(hc p) n -> p hc n", p=P)
    xin = xT_in.rearrange("(kc p) t -> p kc t", p=P); xout = xT_out.rearrange("(kc p) t -> p kc t", p=P)
    cms = []

    def sb(name, shape, dt):
        cm = nc.sbuf_tensor(_uniq(name), shape, dt); t = cm.__enter__(); cms.append(cm); return t

    def ps(name, shape, dt=F32):
        cm = nc.psum_tensor(_uniq(name), shape, dt); t = cm.__enter__(); cms.append(cm); return t

    h_sb = sb("ffs_h", [P, KC, TT], BF16); h_b = [Buf("h%d" % s) for s in range(NS)]
    hid = sb("ffs_hid", [P, HC, TT], BF16); hid_b = [[Buf("hid") for _ in range(NS)] for _ in range(HC)]
    ld = [sb("ffs_ld%d" % i, [P, 512], F32) for i in range(4)]; ld_b = [Buf("ld") for _ in range(4)]
    sq = [sb("ffs_sq%d" % i, [P, 512], BF16) for i in range(2)]; sq_b = [Buf("sq") for _ in range(2)]
    tmp = [sb("ffs_tmp%d" % i, [P, 512], F32) for i in range(2)]; tmp_b = [Buf("tmp") for _ in range(2)]
    rstd = sb("ffs_rstd", [P, 512], F32); rstd_b = Buf("rstd")
    ones = sb("ffs_ones", [P, P], BF16); ones_b = Buf("ones")
    wa = [sb("ffs_wa%d" % i, [P, KC, NB * P], BF16) for i in range(2)]; wa_b = [Buf("wa") for _ in range(2)]
    wg = [sb("ffs_wg%d" % i, [P, KC, NB * P], BF16) for i in range(2)]; wg_b = [Buf("wg") for _ in range(2)]
    w2s = [sb("ffs_w2%d" % i, [P, HC, P], BF16) for i in range(2)]; w2_b = [Buf("w2") for _ in range(2)]
    sg = [sb("ffs_sg%d" % i, [P, 512], F32) for i in range(2)]; sg_b = [Buf("sg") for _ in range(2)]
    xo = [sb("ffs_xo%d" % i, [P, 512], F32) for i in range(2)]; xo_b = [Buf("xo") for _ in range(2)]
    ss_ps = ps("ffs_ss", [P, 512]); ss_b = Buf("ss_ps", psum=True)
    a_ps = [ps("ffs_a%d" % i, [P, 512]) for i in range(2)]; a_b = [Buf("a_ps", psum=True) for _ in range(2)]
    g_ps = [ps("ffs_g%d" % i, [P, 512]) for i in range(2)]; g_b = [Buf("g_ps", psum=True) for _ in range(2)]
    o_ps = [ps("ffs_o%d" % i, [P, 512]) for i in range(2)]; o_b = [Buf("o_ps", psum=True) for _ in range(2)]
    out_bufs = []
    S.op("pool", lambda e: e.memset(ones[:], 1.0), writes=[ones_b])
    cnt = {"sq": 0, "ag": 0, "o": 0, "w": 0, "w2": 0, "ld": 0, "t": 0}

    def load(k, tok):
        i = cnt["ld"] % 4; cnt["ld"] += 1
        S.dma("sp", ld[i][:], xin[:, k, tok], writes=[ld_b[i]])
        return i

    for tt in range(T // TT):
        t0 = tt * TT
        for s in range(NS):
            sl = slice(s * 512, (s + 1) * 512); tok = slice(t0 + s * 512, t0 + (s + 1) * 512)
            for k in range(KC):
                li = load(k, tok)
                i = cnt["sq"] % 2; cnt["sq"] += 1
                S.op("act", lambda e, i=i, li=li: e.activation(out=sq[i][:], in_=ld[li][:], func=AF.Square), reads=[ld_b[li]], writes=[sq_b[i]])
                S.op("pe", lambda e, i=i, k=k: e.matmul(ss_ps[:], lhsT=ones[:], rhs=sq[i][:], start=(k == 0), stop=(k == KC - 1)), reads=[ones_b, sq_b[i]], writes=[ss_b])
            S.op("act", lambda e: e.activation(out=rstd[:], in_=ss_ps[:], func=AF.Sqrt, scale=1.0 / D, bias=eps), reads=[ss_b], writes=[rstd_b])
            S.op("dve", lambda e: e.reciprocal(out=rstd[:], in_=rstd[:]), reads=[rstd_b], writes=[rstd_b])
            for k in range(KC):
                li = load(k, tok)
                i = cnt["t"] % 2; cnt["t"] += 1
                S.op("dve", lambda e, i=i, li=li: e.tensor_tensor(out=tmp[i][:], in0=ld[li][:], in1=rstd[:], op=ALU.mult), reads=[ld_b[li], rstd_b], writes=[tmp_b[i]])
                S.op("act", lambda e, i=i, k=k: e.activation(out=h_sb[:, k, sl], in_=tmp[i][:], func=AF.Identity, scale=gs[:, k:k + 1], bias=sh[:, k:k + 1]),
                     reads=[tmp_b[i], gs_b, sh_b], writes=[h_b[s]])
        for jb in range(HC // NB):
            wi = cnt["w"] % 2; cnt["w"] += 1
            c0 = jb * NB * P
            S.dma("pool", wa[wi][:], w13v[:, :, c0:c0 + NB * P], writes=[wa_b[wi]])
            S.dma("pool", wg[wi][:], w13v[:, :, DFF + c0:DFF + c0 + NB * P], writes=[wg_b[wi]])
            for jj in range(NB):
                j = jb * NB + jj
                for s in range(NS):
                    sl = slice(s * 512, (s + 1) * 512)
                    pi = cnt["ag"] % 2; cnt["ag"] += 1
                    S.mm_group([lambda e, k=k, pi=pi: e.matmul(a_ps[pi][:], lhsT=wa[wi][:, k, jj * P:(jj + 1) * P], rhs=h_sb[:, k, sl], start=(k == 0), stop=(k == KC - 1)) for k in range(KC)],
                               reads=[wa_b[wi], h_b[s]], writes=[a_b[pi]])
                    S.mm_group([lambda e, k=k, pi=pi: e.matmul(g_ps[pi][:], lhsT=wg[wi][:, k, jj * P:(jj + 1) * P], rhs=h_sb[:, k, sl], start=(k == 0), stop=(k == KC - 1)) for k in range(KC)],
                               reads=[wg_b[wi], h_b[s]], writes=[g_b[pi]])
                    S.op("act", lambda e, pi=pi: e.activation(out=sg[pi][:], in_=g_ps[pi][:], func=AF.Silu), reads=[g_b[pi]], writes=[sg_b[pi]])
                    S.op("dve", lambda e, pi=pi, j=j: e.tensor_tensor(out=hid[:, j, sl], in0=a_ps[pi][:], in1=sg[pi][:], op=ALU.mult), reads=[a_b[pi], sg_b[pi]], writes=[hid_b[j][s]])
        for o in range(KC):
            wi = cnt["w2"] % 2; cnt["w2"] += 1
            S.dma("pool", w2s[wi][:], w2v[:, :, o * P:(o + 1) * P], writes=[w2_b[wi]])
            for s in range(NS):
                sl = slice(s * 512, (s + 1) * 512); tok = slice(t0 + s * 512, t0 + (s + 1) * 512)
                pi = cnt["o"] % 2; cnt["o"] += 1
                S.mm_group([lambda e, j=j, pi=pi: e.matmul(o_ps[pi][:], lhsT=w2s[wi][:, j, :], rhs=hid[:, j, sl], start=(j == 0), stop=(j == HC - 1)) for j in range(HC)],
                           reads=[w2_b[wi]] + [hid_b[j][s] for j in range(HC)], writes=[o_b[pi]])
                li = load(o, tok)
                S.op("dve", lambda e, pi=pi, o=o, li=li: e.scalar_tensor_tensor(out=xo[pi][:], in0=o_ps[pi][:], scalar=hg[:, o:o + 1], in1=ld[li][:], op0=ALU.mult, op1=ALU.add),
                     reads=[o_b[pi], ld_b[li], hg_b], writes=[xo_b[pi]])
                ob = Buf("xout"); out_bufs.append(ob)
                S.dma("sp", xout[:, o, tok], xo[pi][:], reads=[xo_b[pi]], writes=[ob])
    return cms, out_bufs


P = 128


def emit_ada(S, nc, c_col, w_ada, b_ada, norms, D, out_tile, out_buf, WB=8):
    KC = D // P
    NF = 9 * KC
    wv = w_ada.rearrange("(kc p) n -> p kc n", p=P)
    cms = []

    def sb(name, shape, dt):
        cm = nc.sbuf_tensor(_uniq(name), shape, dt); t = cm.__enter__(); cms.append(cm); return t

    def ps(name, shape, dt=F32):
        cm = nc.psum_tensor(_uniq(name), shape, dt); t = cm.__enter__(); cms.append(cm); return t

    cact = sb("ada_c", [P, KC], F32); c_b = Buf("cact")
    csig = sb("ada_cs", [P, KC], F32); cs_b = Buf("csig")
    bias = sb("ada_b", [P, NF], F32); b_b = Buf("bias")
    nrm = sb("ada_n", [P, 3, KC], F32); n_b = Buf("nrm")
    raw = sb("ada_raw", [P, NF], F32); raw_b = Buf("raw")
    wt = [sb("ada_w%d" % i, [P, KC, WB * P], F32) for i in range(2)]; w_b = [Buf("w") for _ in range(2)]
    acc = ps("ada_ps", [P, NF]); acc_b = Buf("acc", psum=True)

    S.dma("sp", cact[:], c_col.rearrange("(kc p) o -> p (kc o)", p=P), writes=[c_b], allow_slow_non_contiguous=True)
    S.dma("sp", bias[:], b_ada.rearrange("(n p) -> p n", p=P), writes=[b_b], allow_slow_non_contiguous=True)
    for i, nv in enumerate(norms):
        S.dma("sp", nrm[:, i, :], nv.rearrange("(kc p) -> p kc", p=P), writes=[n_b], allow_slow_non_contiguous=True)
    S.op("act", lambda e: e.activation(out=csig[:], in_=cact[:], func=AF.Sigmoid), reads=[c_b], writes=[cs_b])
    S.op("dve", lambda e: e.tensor_tensor(out=cact[:], in0=cact[:], in1=csig[:], op=ALU.mult), reads=[c_b, cs_b], writes=[c_b])
    for fb in range(NF // WB):
        wi = fb % 2
        S.dma("sp", wt[wi][:], wv[:, :, fb * WB * P:(fb + 1) * WB * P], writes=[w_b[wi]])
        for j in range(WB):
            f = fb * WB + j
            S.mm_group([lambda e, k=k, f=f, j=j: e.matmul(acc[:, f:f + 1], lhsT=wt[wi][:, k, j * P:(j + 1) * P], rhs=cact[:, k:k + 1],
                                                          start=(k == 0), stop=(k == KC - 1)) for k in range(KC)],
                       reads=[w_b[wi], c_b], writes=[acc_b])
    S.op("dve", lambda e: e.tensor_tensor(out=raw[:], in0=acc[:], in1=bias[:], op=ALU.add), reads=[acc_b, b_b], writes=[raw_b])
    r = lambda s: raw[:, s * KC:(s + 1) * KC]
    for i in range(3):
        S.op("dve", lambda e, i=i: e.scalar_tensor_tensor(out=out_tile[:, 3 * i, :], in0=r(3 * i + 1), scalar=1.0, in1=nrm[:, i, :],
                                                         op0=ALU.add, op1=ALU.mult), reads=[raw_b, n_b], writes=[out_buf])
        S.op("dve", lambda e, i=i: e.tensor_copy(out=out_tile[:, 3 * i + 1, :], in_=r(3 * i)), reads=[raw_b], writes=[out_buf])
        gm = 1.0 if i == 1 else 0.5
        S.op("dve", lambda e, i=i, gm=gm: e.tensor_scalar(out=out_tile[:, 3 * i + 2, :], in0=r(3 * i + 2), scalar1=gm, scalar2=None, op0=ALU.mult),
             reads=[raw_b], writes=[out_buf])
    return cms


P = 128
DI, CD, NH, DC = 4096, 6144, 64, 2048
SEG_Z, SEG_X, SEG_DT, SEG_UA, SEG_UG, SEG_G = 0, DI, DI + CD, DI + CD + NH, DI + CD + NH + DC, DI + CD + NH + 2 * DC


def emit_norm_mod(S, nc, sb, ps, xin, h_sb, h_b, gs, sh, gs_b, sh_b, T, D, eps, tag):
    KC = D // P
    xs = [sb(tag + "_x%d" % i, [P, KC, 512], F32) for i in range(2)]; xs_b = [Buf("x") for _ in range(2)]
    sq = [sb(tag + "_sq%d" % i, [P, 512], BF16) for i in range(2)]; sq_b = [Buf("sq") for _ in range(2)]
    tmp = [sb(tag + "_tmp%d" % i, [P, 512], F32) for i in range(2)]; tmp_b = [Buf("tmp") for _ in range(2)]
    rstd = sb(tag + "_rstd", [P, 512], F32); rstd_b = Buf("rstd")
    ones = sb(tag + "_ones", [P, P], BF16); ones_b = Buf("ones")
    ss_ps = ps(tag + "_ss", [P, 512]); ss_b = Buf("ss")
    S.op("pool", lambda e: e.memset(ones[:], 1.0), writes=[ones_b])
    n = 0
    for s in range(T // 512):
        xi = s % 2
        S.dma("sp", xs[xi][:], xin[:, :, s * 512:(s + 1) * 512], writes=[xs_b[xi]])
        sl = slice(s * 512, (s + 1) * 512)
        for k in range(KC):
            i = n % 2; n += 1
            S.op("act", lambda e, i=i, k=k: e.activation(out=sq[i][:], in_=xs[xi][:, k, :], func=AF.Square), reads=[xs_b[xi]], writes=[sq_b[i]])
            S.op("pe", lambda e, i=i, k=k: e.matmul(ss_ps[:], lhsT=ones[:], rhs=sq[i][:], start=(k == 0), stop=(k == KC - 1)),
                 reads=[ones_b, sq_b[i]], writes=[ss_b])
        S.op("act", lambda e: e.activation(out=rstd[:], in_=ss_ps[:], func=AF.Sqrt, scale=1.0 / D, bias=eps), reads=[ss_b], writes=[rstd_b])
        S.op("dve", lambda e: e.reciprocal(out=rstd[:], in_=rstd[:]), reads=[rstd_b], writes=[rstd_b])
        for k in range(KC):
            i = k % 2
            S.op("dve", lambda e, i=i, k=k: e.tensor_tensor(out=tmp[i][:], in0=xs[xi][:, k, :], in1=rstd[:], op=ALU.mult),
                 reads=[xs_b[xi], rstd_b], writes=[tmp_b[i]])
            S.op("act", lambda e, i=i, k=k: e.activation(out=h_sb[:, k, sl], in_=tmp[i][:], func=AF.Identity, scale=gs[:, k:k + 1], bias=sh[:, k:k + 1]),
                 reads=[tmp_b[i], gs_b, sh_b], writes=[h_b[s]])


def emit_mix1(S, nc, xT_in, w_in, gs, sh, gs_b, sh_b, sz, xbc, dtr, u, gat, T, D, eps=1e-6, side=None, PUMP=3):
    KC = D // P
    NT = T // 512
    wv = w_in.rearrange("(kc p) n -> p kc n", p=P)
    xin = xT_in.rearrange("(kc p) t -> p kc t", p=P)
    cms = []

    def sb(name, shape, dt):
        cm = nc.sbuf_tensor(_uniq(name), shape, dt); t = cm.__enter__(); cms.append(cm); return t

    def ps(name, shape, dt=F32):
        cm = nc.psum_tensor(_uniq(name), shape, dt); t = cm.__enter__(); cms.append(cm); return t

    h_sb = sb("m1_h", [P, KC, T], BF16); h_b = [Buf("h") for _ in range(NT)]
    emit_norm_mod(S, nc, sb, ps, xin, h_sb, h_b, gs, sh, gs_b, sh_b, T, D, eps, "m1")

    WB = 4
    NWB = 2 if side is not None else 3
    wt = [sb("m1_w%d" % i, [P, KC, WB * P], BF16) for i in range(NWB)]; wt_b = [Buf("w") for _ in range(NWB)]
    pp = [ps("m1_p%d" % i, [P, 512]) for i in range(4)]; pp_b = [Buf("p", psum=True) for _ in range(4)]
    ev = [sb("m1_ev%d" % i, [P, 512], F32) for i in range(4)]; ev_b = [Buf("ev") for _ in range(4)]
    outs = []
    st = {"w": 0, "p": 0, "e": 0}
    side_gen = [None]

    def pump(k):
        if side_gen[0] is not None:
            for _ in range(k):
                if next(side_gen[0], "done") == "done":
                    side_gen[0] = None
                    break

    def load_w(col0, ncols):
        wi = st["w"] % NWB; st["w"] += 1
        S.dma("pool", wt[wi][:, :, 0:ncols], wv[:, :, col0:col0 + ncols], writes=[wt_b[wi]])
        return wi

    def proj(wi, c0, m, t):
        pi = st["p"] % 4; st["p"] += 1
        sl = slice(t * 512, (t + 1) * 512)
        S.mm_group([lambda e, k=k: e.matmul(pp[pi][0:m, :], lhsT=wt[wi][:, k, c0:c0 + m], rhs=h_sb[:, k, sl], start=(k == 0), stop=(k == KC - 1))
                    for k in range(KC)], reads=[wt_b[wi], h_b[t]], writes=[pp_b[pi]])
        return pi

    def store(dst, row0, m, t, ei):
        ob = Buf("o"); outs.append(ob)
        S.dma("sp", dst[row0:row0 + m, t * 512:(t + 1) * 512], ev[ei][0:m, :], reads=[ev_b[ei]], writes=[ob])

    def simple_seg(col0, width, dst, func):
        for b0 in range(0, width, WB * P):
            nb = min(WB * P, width - b0)
            wi = load_w(col0 + b0, nb)
            for c in range(0, nb, P):
                m = min(P, nb - c)
                for t in range(NT):
                    pi = proj(wi, c, m, t)
                    ei = st["e"] % 4; st["e"] += 1
                    if func is None:
                        S.op("dve", lambda e, pi=pi, ei=ei, m=m: e.tensor_copy(out=ev[ei][0:m, :], in_=pp[pi][0:m, :]), reads=[pp_b[pi]], writes=[ev_b[ei]])
                    else:
                        S.op("act", lambda e, pi=pi, ei=ei, m=m: e.activation(out=ev[ei][0:m, :], in_=pp[pi][0:m, :], func=func), reads=[pp_b[pi]], writes=[ev_b[ei]])
                    store(dst, b0 + c, m, t, ei)
                    pump(PUMP)

    for b0 in range(0, DC, WB * P):
        wa = load_w(SEG_UA + b0, WB * P)
        wg = load_w(SEG_UG + b0, WB * P)
        for c in range(0, WB * P, P):
            for t in range(NT):
                pa = proj(wa, c, P, t)
                pg = proj(wg, c, P, t)
                e1 = st["e"] % 4; st["e"] += 1
                S.op("act", lambda e, pg=pg, e1=e1: e.activation(out=ev[e1][:], in_=pp[pg][:], func=AF.Sigmoid), reads=[pp_b[pg]], writes=[ev_b[e1]])
                S.op("dve", lambda e, pa=pa, e1=e1: e.tensor_tensor(out=ev[e1][:], in0=pp[pa][:], in1=ev[e1][:], op=ALU.mult),
                     reads=[pp_b[pa], ev_b[e1]], writes=[ev_b[e1]])
                store(u, b0 + c, P, t, e1)
    if side is not None:
        S.drain("sp")
        side_gen[0] = side(S, nc, sb, outs)
    simple_seg(SEG_Z, DI, sz, AF.Silu)
    simple_seg(SEG_X, CD, xbc, None)
    simple_seg(SEG_DT, NH, dtr, None)
    simple_seg(SEG_G, 2 * DC, gat, AF.Sigmoid)
    while side_gen[0] is not None:
        pump(64)
    return cms, outs


P = 128


def gen_dwconv(S, nc, sb, src, dst, wts, wts_b, bias, bias_b, C, K, T, func, tag, outs, engines=("dve",), TB=None, t_start=0, t_end=None):
    NCH = C // P
    H = K - 1
    TB = T if TB is None else min(TB, T)
    xin = [sb(tag + "_in%d" % i, [P, H + TB], F32) for i in range(2)]; xin_b = [Buf("cin") for _ in range(2)]
    acc = [sb(tag + "_acc%d" % i, [P, TB], F32) for i in range(2)]; acc_b = [Buf("cacc") for _ in range(2)]
    if "actpool" in engines:
        tmpc = [sb(tag + "_tmp%d" % i, [P, TB], F32) for i in range(2)]; tmpc_b = [Buf("ctmp") for _ in range(2)]
    it = 0
    for c in range(NCH):
        eng = engines[c % len(engines)]
        for t0 in range(t_start, (T if t_end is None else t_end), TB):
            i = it % 2; it += 1
            if t0 == 0:
                S.op("pool", lambda e, i=i: e.memset(xin[i][:, 0:H], 0.0), writes=[xin_b[i]])
                S.dma("sp", xin[i][:, H:H + TB], src[c * P:(c + 1) * P, 0:TB], writes=[xin_b[i]])
            else:
                S.dma("sp", xin[i][:, 0:H + TB], src[c * P:(c + 1) * P, t0 - H:t0 + TB], writes=[xin_b[i]])
            yield
            if eng == "actpool":
                S.op("act", lambda e, i=i, c=c: e.activation(out=acc[i][:], in_=xin[i][:, 0:TB], func=AF.Identity, scale=wts[:, c, 0:1]),
                     reads=[xin_b[i], wts_b], writes=[acc_b[i]])
                yield
                for k in range(1, K):
                    j = k % 2
                    S.op("act", lambda e, i=i, c=c, k=k, j=j: e.activation(out=tmpc[j][:], in_=xin[i][:, k:k + TB], func=AF.Identity, scale=wts[:, c, k:k + 1]),
                         reads=[xin_b[i], wts_b], writes=[tmpc_b[j]])
                    S.op("pool", lambda e, i=i, j=j: e.tensor_tensor(out=acc[i][:], in0=acc[i][:], in1=tmpc[j][:], op=ALU.add),
                         reads=[acc_b[i], tmpc_b[j]], writes=[acc_b[i]])
                    yield
            else:
                S.op(eng, lambda e, i=i, c=c: e.tensor_scalar(out=acc[i][:], in0=xin[i][:, 0:TB], scalar1=wts[:, c, 0:1], scalar2=None, op0=ALU.mult),
                     reads=[xin_b[i], wts_b], writes=[acc_b[i]])
                yield
                for k in range(1, K):
                    S.op(eng, lambda e, i=i, c=c, k=k: e.scalar_tensor_tensor(out=acc[i][:], in0=xin[i][:, k:k + TB], scalar=wts[:, c, k:k + 1], in1=acc[i][:],
                                                                              op0=ALU.mult, op1=ALU.add),
                         reads=[xin_b[i], wts_b, acc_b[i]], writes=[acc_b[i]])
                    yield
            S.op("act", lambda e, i=i, c=c: e.activation(out=acc[i][:], in_=acc[i][:], func=(func or AF.Identity), bias=bias[:, c:c + 1]),
                 reads=[acc_b[i], bias_b], writes=[acc_b[i]])
            yield
            ob = Buf("o"); outs.append(ob)
            S.dma("sp", dst[c * P:(c + 1) * P, t0:t0 + TB], acc[i][:], reads=[acc_b[i]], writes=[ob])
            yield


def emit_dwconv(S, nc, sb, src, dst, wts, wts_b, bias, bias_b, C, K, T, func, tag, engines=("dve",), TB=None):
    outs = []
    for _ in gen_dwconv(S, nc, sb, src, dst, wts, wts_b, bias, bias_b, C, K, T, func, tag, outs, engines=engines, TB=TB):
        pass
    return outs


def load_chan_params(S, nc, sb, w_dram, b_dram, C, K, tag):
    NCH = C // P
    wt = sb(tag + "_w", [P, NCH, K], F32); wb = Buf("cw")
    bt = sb(tag + "_b", [P, NCH], F32); bb = Buf("cb")
    for k in range(K):
        S.dma("sp", wt[:, :, k], w_dram[k].rearrange("(c p) -> p c", p=P), writes=[wb], allow_slow_non_contiguous=True)
    S.dma("sp", bt[:], b_dram.rearrange("(c p) -> p c", p=P), writes=[bb], allow_slow_non_contiguous=True)
    return wt, wb, bt, bb


def emit_mix2(S, nc, xbc, u, xbcs, uc, ssm_conv_w, ssm_conv_b, dw_w, dw_b, T, do31=True):
    cms = []

    def sb(name, shape, dt):
        cm = nc.sbuf_tensor(_uniq(name), shape, dt); t = cm.__enter__(); cms.append(cm); return t

    w4, w4b, b4, b4b = load_chan_params(S, nc, sb, ssm_conv_w, ssm_conv_b, 6144, 4, "c4")
    outs = emit_dwconv(S, nc, sb, xbc, xbcs, w4, w4b, b4, b4b, 6144, 4, T, AF.Silu, "c4")
    if do31:
        w31, w31b, b31, b31b = load_chan_params(S, nc, sb, dw_w, dw_b, 2048, 31, "c31")
        outs += emit_dwconv(S, nc, sb, u, uc, w31, w31b, b31, b31b, 2048, 31, T, None, "c31", engines=("dve",))
    return cms, outs


P = 128
NH, HD, NG, NS_, DI = 64, 64, 8, 128, 4096
HG = NH // NG


def emit_mix3(S, nc, xbcs, dtr, y_out, consts, dt_bias, a_log, d_skip, T, stop_after=None, dbg=None, side=None, PUMP=1):
    NCH = T // P
    cms = []

    def sb(name, shape, dt):
        cm = nc.sbuf_tensor(_uniq(name), shape, dt); t = cm.__enter__(); cms.append(cm); return t

    def ps(name, shape, dt=F32):
        cm = nc.psum_tensor(_uniq(name), shape, dt); t = cm.__enter__(); cms.append(cm); return t

    cst = sb("s3_c", [P, 5 * P], F32); cst_b = Buf("cst")
    S.dma("sp", cst[:], consts[:, :], writes=[cst_b])
    TRI, SU, MASK, ONES, IDENT = [cst[:, i * P:(i + 1) * P] for i in range(5)]
    hp = sb("s3_hp", [NH, 4], F32); hp_b = Buf("hp")
    S.dma("sp", hp[:, 0:1], dt_bias.rearrange("(h o) -> h o", o=1), writes=[hp_b])
    S.dma("sp", hp[:, 1:2], a_log.rearrange("(h o) -> h o", o=1), writes=[hp_b])
    S.op("act", lambda e: e.activation(out=hp[:, 2:3], in_=hp[:, 1:2], func=AF.Exp), reads=[hp_b], writes=[hp_b])
    S.op("dve", lambda e: e.tensor_scalar(out=hp[:, 1:2], in0=hp[:, 2:3], scalar1=-1.0, scalar2=None, op0=ALU.mult), reads=[hp_b], writes=[hp_b])
    dsk = sb("s3_dsk", [1, NH], F32); dsk_b = Buf("dsk")
    S.dma("sp", dsk[:], d_skip.rearrange("(o h) -> o h", o=1), writes=[dsk_b])
    Drow = sb("s3_Drow", [P, DI], F32); Drow_b = Buf("Drow")
    one1 = sb("s3_one1", [1, P], F32); one1_b = Buf("one1")
    S.op("pool", lambda e: e.memset(one1[:], 1.0), writes=[one1_b])
    dskb = sb("s3_dskb", [P, NH], F32); dskb_b = Buf("dskb")
    pm = ps("s3_pm", [P, 512]); pm_b = Buf("pm", psum=True)
    S.op("pe", lambda e: e.matmul(pm[:, 0:NH], lhsT=one1[:], rhs=dsk[:], start=True, stop=True), reads=[one1_b, dsk_b], writes=[pm_b])
    S.op("dve", lambda e: e.tensor_copy(out=dskb[:], in_=pm[:, 0:NH]), reads=[pm_b], writes=[dskb_b])
    for h in range(NH):
        S.op("act", lambda e, h=h: e.activation(out=Drow[:, h * HD:(h + 1) * HD], in_=cst[:, 0:HD], func=AF.Identity, scale=0.0, bias=dskb[:, h:h + 1]),
             reads=[cst_b, dskb_b], writes=[Drow_b])

    xsT = [sb("s3_xsT%d" % i, [P, 4, P], F32) for i in range(2)]; xsT_b = [Buf("xsT") for _ in range(2)]
    bcT = sb("s3_bcT", [P, 2 * NG, P], F32); bcT_b = Buf("bcT")
    Bt = sb("s3_Bt", [P, NG, P], BF16); Bt_b = Buf("Bt"); Ct = sb("s3_Ct", [P, NG, P], BF16); Ct_b = Buf("Ct")
    Btm = sb("s3_Btm", [P, NG, P], BF16); Btm_b = Buf("Btm")
    dtf = sb("s3_dtf", [NH, 2, P], F32); dtf_b = Buf("dtf")
    dtm = sb("s3_dtm", [P, 2 * NH], F32); dtm_b = Buf("dtm")
    dec = sb("s3_dec", [P, 3 * NH], F32); dec_b = Buf("dec")
    acs = sb("s3_acs", [P, 2 * NH], F32); acs_b = Buf("acs")
    xs_tm = sb("s3_xs", [P, DI], F32); xs_b = Buf("xs_tm")
    xdt = sb("s3_xdt", [P, DI], BF16); xdt_b = Buf("xdt")
    xdte = sb("s3_xdte", [P, DI], BF16); xdte_b = Buf("xdte")
    y_tm = sb("s3_y", [P, DI], F32); y_b = Buf("y_tm")
    S32 = sb("s3_S32", [P, DI], F32); S32_b = Buf("S32")
    Sbf = sb("s3_Sbf", [P, DI], BF16); Sbf_b = Buf("Sbf")
    L8 = [sb("s3_L%d" % i, [P, 4, P], F32) for i in range(2)]; L8_b = [Buf("L") for _ in range(2)]
    dte = sb("s3_dte", [P, NH], F32); dte_b = Buf("dte")
    mskb = sb("s3_mskb", [P, 2 * P], BF16); mskb_b = Buf("mskb")
    S.op("act", lambda e: e.activation(out=mskb[:], in_=cst[:, 0:2 * P], func=AF.Identity), reads=[cst_b], writes=[mskb_b])
    TRIb, SUb = mskb[:, 0:P], mskb[:, P:2 * P]
    dhl = sb("s3_dhl", [P, 2 * NH], BF16); dhl_b = Buf("dhl")
    R8h = [sb("s3_Rh%d" % i, [P, 4, P], BF16) for i in range(2)]; R8l = [sb("s3_Rl%d" % i, [P, 4, P], BF16) for i in range(2)]
    CBm2 = [sb("s3_CBm%d" % i, [P, P], F32) for i in range(2)]; CBm2_b = [Buf("CBm") for _ in range(2)]
    Ex = [sb("s3_Ex%d" % i, [P, 512], F32) for i in range(2)]; Ex_b = [Buf("Ex") for _ in range(2)]
    MT2 = [sb("s3_MT%d" % i, [P, HG, P], BF16) for i in range(2)]; MT2_b = [Buf("MT") for _ in range(2)]
    yev = [sb("s3_yev%d" % i, [P, 512], F32) for i in range(2)]; yev_b = [Buf("yev") for _ in range(2)]
    tr = [ps("s3_tr%d" % i, [P, 512]) for i in range(2)]; tr_b = [Buf("tr", psum=True) for _ in range(2)]
    sg = [ps("s3_sg%d" % i, [P, 512]) for i in range(2)]; sg_b = [Buf("sg", psum=True) for _ in range(2)]
    yd = ps("s3_yd", [P, 512]); yd_b = Buf("yd", psum=True)
    yo = ps("s3_yo", [P, 512]); yo_b = Buf("yo", psum=True)
    stp = ps("s3_st", [P, 512]); stp_b = Buf("stp", psum=True)
    yd2 = [yd, tr[0]]; yd2_b = [yd_b, tr_b[0]]
    yo2 = [yo, tr[1]]; yo2_b = [yo_b, tr_b[1]]
    outs = []
    side_gen = side(S, nc, sb, outs) if side is not None else None

    def pump(k):
        if side_gen is not None:
            for _ in range(k):
                if next(side_gen, "done") == "done":
                    break

    def dv(fn, reads=(), writes=()):
        S.op("dve", fn, reads=reads, writes=writes)
        pump(PUMP)
    S.op("pool", lambda e: e.memset(S32[:], 0.0), writes=[S32_b])
    S.op("pool", lambda e: e.memset(Sbf[:], 0.0), writes=[Sbf_b])
    n = {"tr": 0, "sg": 0, "L": 0, "Ex": 0, "yev": 0, "x": 0}
    yout = y_out.rearrange("(c p) t -> p c t", p=P)

    for c in range(NCH):
        tok = slice(c * P, (c + 1) * P)
        S.dma("sp", dtf[:, 0, :], dtr[:, tok], writes=[dtf_b])
        S.op("act", lambda e: e.activation(out=dtf[:, 0, :], in_=dtf[:, 0, :], func=AF.Exp, bias=hp[:, 0:1]), reads=[dtf_b, hp_b], writes=[dtf_b])
        S.op("act", lambda e: e.activation(out=dtf[:, 0, :], in_=dtf[:, 0, :], func=AF.Ln, bias=1.0), reads=[dtf_b], writes=[dtf_b])
        dv(lambda e: e.tensor_scalar(out=dtf[:, 1, :], in0=dtf[:, 0, :], scalar1=hp[:, 1:2], scalar2=None, op0=ALU.mult), reads=[dtf_b, hp_b], writes=[dtf_b])
        ti = n["tr"] % 2; n["tr"] += 1
        S.op("pe", lambda e: e.transpose(tr[ti][:, 0:NH], dtf[:, 0, :], IDENT[0:NH, 0:NH]), reads=[dtf_b, cst_b], writes=[tr_b[ti]])
        S.op("pe", lambda e: e.transpose(tr[ti][:, NH:2 * NH], dtf[:, 1, :], IDENT[0:NH, 0:NH]), reads=[dtf_b, cst_b, tr_b[ti]], writes=[tr_b[ti]])
        dv(lambda e: e.tensor_copy(out=dtm[:], in_=tr[ti][:, 0:2 * NH]), reads=[tr_b[ti]], writes=[dtm_b])
        S.op("act", lambda e: e.activation(out=dhl[:, 0:NH], in_=dtm[:, NH:2 * NH], func=AF.Identity), reads=[dtm_b], writes=[dhl_b])
        dv(lambda e: e.tensor_tensor(out=dhl[:, NH:2 * NH], in0=dtm[:, NH:2 * NH], in1=dhl[:, 0:NH], op=ALU.subtract), reads=[dtm_b, dhl_b], writes=[dhl_b])
        S.op("pe", lambda e: e.matmul(pm[:, 0:NH], lhsT=TRI, rhs=dtm[:, NH:2 * NH], start=True, stop=True), reads=[cst_b, dtm_b], writes=[pm_b])
        S.op("pe", lambda e: e.matmul(pm[:, NH:2 * NH], lhsT=ONES, rhs=dtm[:, NH:2 * NH], start=True, stop=True), reads=[cst_b, dtm_b, pm_b], writes=[pm_b])
        dv(lambda e: e.tensor_copy(out=acs[:], in_=pm[:, 0:2 * NH]), reads=[pm_b], writes=[acs_b])
        S.op("act", lambda e: e.activation(out=dec[:, 0:NH], in_=acs[:, 0:NH], func=AF.Exp), reads=[acs_b], writes=[dec_b])
        dv(lambda e: e.tensor_tensor(out=acs[:, 0:NH], in0=acs[:, NH:2 * NH], in1=acs[:, 0:NH], op=ALU.subtract), reads=[acs_b, dec_b], writes=[acs_b])
        S.op("act", lambda e: e.activation(out=dec[:, NH:2 * NH], in_=acs[:, 0:NH], func=AF.Exp), reads=[acs_b], writes=[dec_b])
        S.op("act", lambda e: e.activation(out=dec[:, 2 * NH:3 * NH], in_=acs[:, NH:2 * NH], func=AF.Exp), reads=[acs_b], writes=[dec_b])
        def cut():
            S.drain("sp")
            ob = Buf("o"); outs.append(ob)
            S.dma("sp", dbg[:, c * 3 * NH:(c + 1) * 3 * NH], dec[:], reads=[dec_b], writes=[ob])
        if stop_after == "dec":
            cut(); continue
        S.dma("sp", bcT[:], xbcs[DI:DI + 2 * NG * NS_, tok].rearrange("(g p) t -> p g t", p=P), writes=[bcT_b])
        S.op("act", lambda e: e.activation(out=Bt[:], in_=bcT[:, 0:NG, :], func=AF.Identity), reads=[bcT_b], writes=[Bt_b])
        S.op("act", lambda e: e.activation(out=Ct[:], in_=bcT[:, NG:2 * NG, :], func=AF.Identity), reads=[bcT_b], writes=[Ct_b])
        for g4 in range(NG // 4):
            ti = n["tr"] % 2; n["tr"] += 1
            for q in range(4):
                S.op("pe", lambda e, q=q: e.transpose(tr[ti][:, q * P:(q + 1) * P], bcT[:, g4 * 4 + q, :], IDENT), reads=[bcT_b, cst_b, tr_b[ti]], writes=[tr_b[ti]])
            dv(lambda e: e.tensor_copy(out=Btm[:, g4 * 4:(g4 + 1) * 4, :], in_=tr[ti][:].rearrange("p (g n) -> p g n", g=4)), reads=[tr_b[ti]], writes=[Btm_b])
        if stop_after == "bc":
            cut(); continue
        for f4 in range(DI // 512):
            xi = n["x"] % 2; n["x"] += 1
            S.dma("sp", xsT[xi][:], xbcs[f4 * 512:(f4 + 1) * 512, tok].rearrange("(q p) t -> p q t", p=P), writes=[xsT_b[xi]])
            ti = n["tr"] % 2; n["tr"] += 1
            for q in range(4):
                S.op("pe", lambda e, q=q: e.transpose(tr[ti][:, q * P:(q + 1) * P], xsT[xi][:, q, :], IDENT), reads=[xsT_b[xi], cst_b, tr_b[ti]], writes=[tr_b[ti]])
            S.op("act", lambda e: e.activation(out=xs_tm[:, f4 * 512:(f4 + 1) * 512], in_=tr[ti][:], func=AF.Identity), reads=[tr_b[ti]], writes=[xs_b])
        dv(lambda e: e.tensor_tensor(out=dte[:], in0=dtm[:, 0:NH], in1=dec[:, NH:2 * NH], op=ALU.mult), reads=[dtm_b, dec_b], writes=[dte_b])
        h3 = lambda ap: ap.rearrange("p (h d) -> p h d", d=HD)
        bc = lambda ap, nh: ap.unsqueeze(2).to_broadcast([P, nh, HD])
        dv(lambda e: e.tensor_tensor(out=h3(xdt[:]), in0=h3(xs_tm[:]), in1=bc(dtm[:, 0:NH], NH), op=ALU.mult), reads=[xs_b, dtm_b], writes=[xdt_b])
        dv(lambda e: e.tensor_tensor(out=h3(xdte[:]), in0=h3(xs_tm[:]), in1=bc(dte[:], NH), op=ALU.mult), reads=[xs_b, dte_b], writes=[xdte_b])
        if stop_after == "xs":
            cut(); continue
        def stage_A(g):
            gc = slice(g * 512, (g + 1) * 512); pb = g % 2
            CBm, CBm_b, MT, MT_b, yo_, yo_b_ = CBm2[pb], CBm2_b[pb], MT2[pb], MT2_b[pb], yo2[pb], yo2_b[pb]
            S.op("pe", lambda e: e.matmul(pm[:, 2 * NH:2 * NH + P], lhsT=Bt[:, g, :], rhs=Ct[:, g, :], start=True, stop=True), reads=[Bt_b, Ct_b, pm_b], writes=[pm_b])
            dv(lambda e: e.tensor_tensor(out=CBm[:], in0=pm[:, 2 * NH:2 * NH + P], in1=MASK, op=ALU.mult), reads=[pm_b, cst_b], writes=[CBm_b])
            S.op("pe", lambda e: e.matmul(yo_[:], lhsT=Ct[:, g, :], rhs=Sbf[:, gc], start=True, stop=True), reads=[Ct_b, Sbf_b], writes=[yo_b_])
            for h4 in range(2):
                si = n["sg"] % 2; n["sg"] += 1
                h0 = g * HG + h4 * 4
                li = n["L"] % 2; n["L"] += 1
                dv(lambda e, li=li, h0=h0: e.tensor_tensor(out=R8h[li][:], in0=TRIb.unsqueeze(1).to_broadcast([P, 4, P]),
                                                                    in1=dhl[:, h0:h0 + 4].unsqueeze(2).to_broadcast([P, 4, P]), op=ALU.mult),
                     reads=[mskb_b, dhl_b], writes=[L8_b[li]])
                dv(lambda e, li=li, h0=h0: e.tensor_tensor(out=R8l[li][:], in0=TRIb.unsqueeze(1).to_broadcast([P, 4, P]),
                                                                    in1=dhl[:, NH + h0:NH + h0 + 4].unsqueeze(2).to_broadcast([P, 4, P]), op=ALU.mult),
                     reads=[mskb_b, dhl_b, L8_b[li]], writes=[L8_b[li]])
                S.mm_group([lambda e, li=li: e.matmul(sg[si][:], lhsT=SUb, rhs=R8h[li][:].rearrange("p h l -> p (h l)"), start=True, stop=False),
                            lambda e, li=li: e.matmul(sg[si][:], lhsT=SUb, rhs=R8l[li][:].rearrange("p h l -> p (h l)"), start=False, stop=True)],
                           reads=[L8_b[li], mskb_b], writes=[sg_b[si]])
                ei = n["Ex"] % 2; n["Ex"] += 1
                S.op("act", lambda e, ei=ei: e.activation(out=Ex[ei][:], in_=sg[si][:], func=AF.Exp), reads=[sg_b[si]], writes=[Ex_b[ei]])
                dv(lambda e, ei=ei: e.tensor_tensor(out=MT[:, h4 * 4:(h4 + 1) * 4, :], in0=Ex[ei][:].rearrange("p (q l) -> p q l", q=4),
                                                              in1=CBm[:].unsqueeze(1).to_broadcast([P, 4, P]), op=ALU.mult), reads=[Ex_b[ei], CBm_b], writes=[MT_b])

        def stage_B(g):
            gc = slice(g * 512, (g + 1) * 512); pb = g % 2
            MT, MT_b, yo_, yo_b_, yd_, yd_b_ = MT2[pb], MT2_b[pb], yo2[pb], yo2_b[pb], yd2[pb], yd2_b[pb]
            for hh in range(HG):
                h = g * HG + hh
                S.op("pe", lambda e, h=h, hh=hh: e.matmul(yd_[:, hh * HD:(hh + 1) * HD], lhsT=MT[:, hh, :], rhs=xdt[:, h * HD:(h + 1) * HD], start=True, stop=True),
                     reads=[MT_b, xdt_b, yd_b_], writes=[yd_b_])
            dv(lambda e: e.tensor_tensor(out=h3(y_tm[:, gc]), in0=h3(yo_[:]), in1=bc(dec[:, g * HG:(g + 1) * HG], HG), op=ALU.mult), reads=[yo_b_, dec_b], writes=[y_b])
            dv(lambda e: e.tensor_tensor(out=y_tm[:, gc], in0=yd_[:], in1=y_tm[:, gc], op=ALU.add), reads=[yd_b_, y_b], writes=[y_b])
            S.op("pe", lambda e: e.matmul(stp[:], lhsT=Btm[:, g, :], rhs=xdte[:, gc], start=True, stop=True), reads=[Btm_b, xdte_b], writes=[stp_b])
            dv(lambda e: e.tensor_tensor(out=h3(S32[:, gc]), in0=h3(S32[:, gc]), in1=bc(dec[:, 2 * NH + g * HG:2 * NH + (g + 1) * HG], HG), op=ALU.mult),
                 reads=[S32_b, dec_b, yo_b_], writes=[S32_b])
            dv(lambda e: e.tensor_tensor(out=S32[:, gc], in0=stp[:], in1=S32[:, gc], op=ALU.add), reads=[stp_b, S32_b], writes=[S32_b])
            S.op("act", lambda e: e.activation(out=Sbf[:, gc], in_=S32[:, gc], func=AF.Identity), reads=[S32_b, yo_b_], writes=[Sbf_b])

        stage_A(0)
        for g in range(NG):
            if g + 1 < NG:
                stage_A(g + 1)
            stage_B(g)
        if stop_after == "grp":
            cut(); continue
        dv(lambda e: e.tensor_tensor(out=xs_tm[:], in0=xs_tm[:], in1=Drow[:], op=ALU.mult), reads=[xs_b, Drow_b], writes=[xs_b])
        dv(lambda e: e.tensor_tensor(out=y_tm[:], in0=y_tm[:], in1=xs_tm[:], op=ALU.add), reads=[y_b, xs_b], writes=[y_b])
        for f4 in range(DI // 512):
            ti = n["tr"] % 2; n["tr"] += 1
            for q in range(4):
                fc = f4 * 4 + q
                S.op("pe", lambda e, q=q, fc=fc: e.transpose(tr[ti][:, q * P:(q + 1) * P], y_tm[:, fc * P:(fc + 1) * P], IDENT), reads=[y_b, cst_b, tr_b[ti]], writes=[tr_b[ti]])
            yi = n["yev"] % 2; n["yev"] += 1
            S.op("act", lambda e, yi=yi: e.activation(out=yev[yi][:], in_=tr[ti][:], func=AF.Identity), reads=[tr_b[ti]], writes=[yev_b[yi]])
            ob = Buf("o"); outs.append(ob)
            S.dma("sp", yout[:, f4 * 4:(f4 + 1) * 4, tok], yev[yi][:].rearrange("p (q t) -> p q t", q=4), reads=[yev_b[yi]], writes=[ob])
    if side_gen is not None:
        for _ in side_gen:
            pass
    return cms, outs


P = 128
DI, DC = 4096, 2048


def emit_mix4(S, nc, y, sz, uc, gat, xT_in, xT_out, ssm_norm_w, w_ssm_out, ln_g, ln_b, w_pw2, b_pw2, w_o, g2, g2_b, T, D, eps=1e-6, dbg=None):
    KI = DI // P
    KC = D // P
    cms = []

    def sb(name, shape, dt):
        cm = nc.sbuf_tensor(_uniq(name), shape, dt); t = cm.__enter__(); cms.append(cm); return t

    def ps(name, shape, dt=F32):
        cm = nc.psum_tensor(_uniq(name), shape, dt); t = cm.__enter__(); cms.append(cm); return t

    yv = y.rearrange("(k p) t -> p k t", p=P); szv = sz.rearrange("(k p) t -> p k t", p=P)
    ucv = uc.rearrange("(k p) t -> p k t", p=P); gv = gat.rearrange("(k p) t -> p k t", p=P)
    xin = xT_in.rearrange("(k p) t -> p k t", p=P); xout = xT_out.rearrange("(k p) t -> p k t", p=P)
    wsv = w_ssm_out.rearrange("(k p) n -> p k n", p=P); wpv = w_pw2.rearrange("(k p) n -> p k n", p=P); wov = w_o.rearrange("(k p) n -> p k n", p=P)

    prm = sb("s4_prm", [P, KI + 3 * KC], F32); prm_b = Buf("prm")
    S.dma("sp", prm[:, 0:KI], ssm_norm_w.rearrange("(k p) -> p k", p=P), writes=[prm_b], allow_slow_non_contiguous=True)
    for i, v in enumerate([ln_g, ln_b, b_pw2]):
        S.dma("sp", prm[:, KI + i * KC:KI + (i + 1) * KC], v.rearrange("(k p) -> p k", p=P), writes=[prm_b], allow_slow_non_contiguous=True)
    NW = lambda k: prm[:, k:k + 1]
    LG = lambda k: prm[:, KI + k:KI + k + 1]
    LB = lambda k: prm[:, KI + KC + k:KI + KC + k + 1]
    BP = lambda k: prm[:, KI + 2 * KC + k:KI + 2 * KC + k + 1]

    ones_b16 = sb("s4_ones", [P, P], BF16); ones_f32 = sb("s4_onesf", [P, P], F32); on_b = Buf("ones")
    S.op("pool", lambda e: e.memset(ones_b16[:], 1.0), writes=[on_b])
    S.op("pool", lambda e: e.memset(ones_f32[:], 1.0), writes=[on_b])
    ygw = sb("s4_ygw", [P, KI, 512], BF16); ygw_b = Buf("ygw")
    ucs = sb("s4_uc", [P, KC, 512], F32); ucs_b = Buf("ucs")
    ua = sb("s4_ua", [P, KC, 512], BF16); ua_b = Buf("ua")
    mt = sb("s4_m", [P, KC, 512], BF16); m_b = [Buf("m%d" % k) for k in range(KC)]
    ld = [sb("s4_ld%d" % i, [P, 512], F32) for i in range(4)]; ld_b = [Buf("ld") for _ in range(4)]
    t1 = [sb("s4_t%d" % i, [P, 512], F32) for i in range(2)]; t1_b = [Buf("t1") for _ in range(2)]
    sq = [sb("s4_sq%d" % i, [P, 512], BF16) for i in range(2)]; sq_b = [Buf("sq") for _ in range(2)]
    rs_s = sb("s4_rs", [P, 512], F32); rs_b = Buf("rs")
    mu = sb("s4_mu", [P, 512], F32); mu_b = Buf("mu")
    rc = sb("s4_rc", [P, 512], F32); rc_b = Buf("rc")
    ev = [sb("s4_ev%d" % i, [P, 512], F32) for i in range(2)]; ev_b = [Buf("ev") for _ in range(2)]
    xo = [sb("s4_xo%d" % i, [P, 512], F32) for i in range(2)]; xo_b = [Buf("xo") for _ in range(2)]
    ws = [sb("s4_ws%d" % i, [P, KI, P], BF16) for i in range(2)]; ws_b = [Buf("ws") for _ in range(2)]
    wp = [sb("s4_wp%d" % i, [P, KC, P], BF16) for i in range(2)]; wp_b = [Buf("wp") for _ in range(2)]
    wo = [sb("s4_wo%d" % i, [P, KC, P], BF16) for i in range(2)]; wo_b = [Buf("wo") for _ in range(2)]
    st_y = ps("s4_sty", [P, 512]); sty_b = Buf("sty", psum=True)
    st_u = ps("s4_stu", [P, 512]); stu_b = Buf("stu", psum=True)
    st_u2 = ps("s4_stu2", [P, 512]); stu2_b = Buf("stu2", psum=True)
    pa = [ps("s4_pa%d" % i, [P, 512]) for i in range(2)]; pa_b = [Buf("pa", psum=True) for _ in range(2)]
    po = [ps("s4_po%d" % i, [P, 512]) for i in range(2)]; po_b = [Buf("po", psum=True) for _ in range(2)]
    outs = []
    n = {"ld": 0, "t": 0, "sq": 0, "ev": 0, "xo": 0, "pa": 0, "po": 0, "ws": 0, "wp": 0, "wo": 0}

    def load(src, k, tok):
        i = n["ld"] % 4; n["ld"] += 1
        S.dma("sp", ld[i][:], src[:, k, tok], writes=[ld_b[i]])
        return i

    for tt in range(T // 512):
        tok = slice(tt * 512, (tt + 1) * 512)
        for k in range(KI):
            iy = load(yv, k, tok); iz = load(szv, k, tok)
            ti = n["t"] % 2; n["t"] += 1
            S.op("dve", lambda e, ti=ti, iy=iy, iz=iz: e.tensor_tensor(out=t1[ti][:], in0=ld[iy][:], in1=ld[iz][:], op=ALU.mult),
                 reads=[ld_b[iy], ld_b[iz]], writes=[t1_b[ti]])
            si = n["sq"] % 2; n["sq"] += 1
            S.op("act", lambda e, ti=ti, si=si: e.activation(out=sq[si][:], in_=t1[ti][:], func=AF.Square), reads=[t1_b[ti]], writes=[sq_b[si]])
            S.op("pe", lambda e, si=si, k=k: e.matmul(st_y[:], lhsT=ones_b16[:], rhs=sq[si][:], start=(k == 0), stop=(k == KI - 1)),
                 reads=[on_b, sq_b[si]], writes=[sty_b])
            S.op("dve", lambda e, ti=ti, k=k: e.tensor_scalar(out=ygw[:, k, :], in0=t1[ti][:], scalar1=NW(k), scalar2=None, op0=ALU.mult),
                 reads=[t1_b[ti], prm_b], writes=[ygw_b])
        S.op("act", lambda e: e.activation(out=rs_s[:], in_=st_y[:], func=AF.Sqrt, scale=1.0 / DI, bias=eps), reads=[sty_b], writes=[rs_b])
        S.op("dve", lambda e: e.reciprocal(out=rs_s[:], in_=rs_s[:]), reads=[rs_b], writes=[rs_b])
        S.dma("sp", ucs[:], ucv[:, :, tok], writes=[ucs_b])
        for k in range(KC):
            si = n["sq"] % 2; n["sq"] += 1
            S.op("act", lambda e, si=si, k=k: e.activation(out=sq[si][:], in_=ucs[:, k, :], func=AF.Square), reads=[ucs_b], writes=[sq_b[si]])
            S.op("pe", lambda e, si=si, k=k: e.matmul(st_u2[:], lhsT=ones_b16[:], rhs=sq[si][:], start=(k == 0), stop=(k == KC - 1)),
                 reads=[on_b, sq_b[si]], writes=[stu2_b])
            S.op("pe", lambda e, k=k: e.matmul(st_u[:], lhsT=ones_f32[:], rhs=ucs[:, k, :], start=(k == 0), stop=(k == KC - 1)),
                 reads=[on_b, ucs_b], writes=[stu_b])
        S.op("dve", lambda e: e.tensor_scalar(out=mu[:], in0=st_u[:], scalar1=1.0 / DC, scalar2=None, op0=ALU.mult), reads=[stu_b], writes=[mu_b])
        S.op("dve", lambda e: e.tensor_tensor(out=rc[:], in0=mu[:], in1=mu[:], op=ALU.mult), reads=[mu_b], writes=[rc_b])
        S.op("dve", lambda e: e.scalar_tensor_tensor(out=rc[:], in0=st_u2[:], scalar=1.0 / DC, in1=rc[:], op0=ALU.mult, op1=ALU.subtract),
             reads=[stu2_b, rc_b], writes=[rc_b])
        S.op("act", lambda e: e.activation(out=rc[:], in_=rc[:], func=AF.Sqrt, bias=eps), reads=[rc_b], writes=[rc_b])
        S.op("dve", lambda e: e.reciprocal(out=rc[:], in_=rc[:]), reads=[rc_b], writes=[rc_b])
        for k in range(KC):
            ti = n["t"] % 2; n["t"] += 1
            S.op("dve", lambda e, ti=ti, k=k: e.tensor_tensor(out=t1[ti][:], in0=ucs[:, k, :], in1=mu[:], op=ALU.subtract), reads=[ucs_b, mu_b], writes=[t1_b[ti]])
            S.op("dve", lambda e, ti=ti: e.tensor_tensor(out=t1[ti][:], in0=t1[ti][:], in1=rc[:], op=ALU.mult), reads=[t1_b[ti], rc_b], writes=[t1_b[ti]])
            S.op("act", lambda e, ti=ti, k=k: e.activation(out=ua[:, k, :], in_=t1[ti][:], func=AF.Silu, scale=LG(k), bias=LB(k)),
                 reads=[t1_b[ti], prm_b], writes=[ua_b])
        for o in range(KC):
            oc = slice(o * P, (o + 1) * P)
            wi = n["ws"] % 2; n["ws"] += 1
            S.dma("pool", ws[wi][:], wsv[:, :, oc], writes=[ws_b[wi]])
            wj = n["wp"] % 2; n["wp"] += 1
            S.dma("pool", wp[wj][:], wpv[:, :, oc], writes=[wp_b[wj]])
            p1 = n["pa"] % 2; n["pa"] += 1
            S.mm_group([lambda e, k=k: e.matmul(pa[p1][:], lhsT=ws[wi][:, k, :], rhs=ygw[:, k, :], start=(k == 0), stop=(k == KI - 1)) for k in range(KI)],
                       reads=[ws_b[wi], ygw_b], writes=[pa_b[p1]])
            p2 = n["pa"] % 2; n["pa"] += 1
            S.mm_group([lambda e, k=k: e.matmul(pa[p2][:], lhsT=wp[wj][:, k, :], rhs=ua[:, k, :], start=(k == 0), stop=(k == KC - 1)) for k in range(KC)],
                       reads=[wp_b[wj], ua_b], writes=[pa_b[p2]])
            igs = load(gv, o, tok); igc = load(gv, KC + o, tok)
            e1 = n["ev"] % 2; n["ev"] += 1
            S.op("dve", lambda e, e1=e1, p1=p1: e.tensor_tensor(out=ev[e1][:], in0=pa[p1][:], in1=rs_s[:], op=ALU.mult), reads=[pa_b[p1], rs_b], writes=[ev_b[e1]])
            S.op("dve", lambda e, e1=e1, igs=igs: e.tensor_tensor(out=ev[e1][:], in0=ev[e1][:], in1=ld[igs][:], op=ALU.mult), reads=[ev_b[e1], ld_b[igs]], writes=[ev_b[e1]])
            e2 = n["ev"] % 2; n["ev"] += 1
            S.op("act", lambda e, e2=e2, p2=p2, o=o: e.activation(out=ev[e2][:], in_=pa[p2][:], func=AF.Identity, bias=BP(o)), reads=[pa_b[p2], prm_b], writes=[ev_b[e2]])
            S.op("dve", lambda e, e2=e2, igc=igc: e.tensor_tensor(out=ev[e2][:], in0=ev[e2][:], in1=ld[igc][:], op=ALU.mult), reads=[ev_b[e2], ld_b[igc]], writes=[ev_b[e2]])
            S.op("dve", lambda e, e1=e1, e2=e2, o=o: e.tensor_tensor(out=mt[:, o, :], in0=ev[e1][:], in1=ev[e2][:], op=ALU.add), reads=[ev_b[e1], ev_b[e2]], writes=[m_b[o]])
        if dbg is not None:
            for k in range(KC):
                ob = Buf("o"); outs.append(ob)
                S.dma("pool", dbg["ua"].rearrange("(k p) t -> p k t", p=P)[:, k, tok], ua[:, k, :], reads=[ua_b], writes=[ob])
                ob = Buf("o"); outs.append(ob)
                S.dma("pool", dbg["m"].rearrange("(k p) t -> p k t", p=P)[:, k, tok], mt[:, k, :], reads=[m_b[k]], writes=[ob])
            for k in range(KI):
                ob = Buf("o"); outs.append(ob)
                S.dma("pool", dbg["ygw"].rearrange("(k p) t -> p k t", p=P)[:, k, tok], ygw[:, k, :], reads=[ygw_b], writes=[ob])
            ob = Buf("o"); outs.append(ob)
            S.dma("sp", dbg["rs"][:, tok], rs_s[:], reads=[rs_b], writes=[ob])
        for o in range(KC):
            oc = slice(o * P, (o + 1) * P)
            wk = n["wo"] % 2; n["wo"] += 1
            S.dma("pool", wo[wk][:], wov[:, :, oc], writes=[wo_b[wk]])
            p3 = n["po"] % 2; n["po"] += 1
            S.mm_group([lambda e, k=k: e.matmul(po[p3][:], lhsT=wo[wk][:, k, :], rhs=mt[:, k, :], start=(k == 0), stop=(k == KC - 1)) for k in range(KC)],
                       reads=[wo_b[wk]] + m_b, writes=[po_b[p3]])
            ix = load(xin, o, tok)
            xi = n["xo"] % 2; n["xo"] += 1
            S.op("dve", lambda e, xi=xi, p3=p3, ix=ix, o=o: e.scalar_tensor_tensor(out=xo[xi][:], in0=po[p3][:], scalar=g2[:, o:o + 1], in1=ld[ix][:], op0=ALU.mult, op1=ALU.add),
                 reads=[po_b[p3], ld_b[ix], g2_b], writes=[xo_b[xi]])
            ob = Buf("o"); outs.append(ob)
            S.dma("sp", xout[:, o, tok], xo[xi][:], reads=[xo_b[xi]], writes=[ob])
    return cms, outs


P = 128


def emit_final(S, nc, xT_in, yT, gvec, T, D, eps=1e-6):
    KC = D // P
    cms = []

    def sb(name, shape, dt):
        cm = nc.sbuf_tensor(_uniq(name), shape, dt); t = cm.__enter__(); cms.append(cm); return t

    def ps(name, shape, dt=F32):
        cm = nc.psum_tensor(_uniq(name), shape, dt); t = cm.__enter__(); cms.append(cm); return t

    xin = xT_in.rearrange("(k p) t -> p k t", p=P); yout = yT.rearrange("(k p) t -> p k t", p=P)
    g = sb("fn_g", [P, KC], F32); g_b = Buf("g")
    S.dma("sp", g[:], gvec.rearrange("(k p) -> p k", p=P), writes=[g_b], allow_slow_non_contiguous=True)
    ones = sb("fn_ones", [P, P], BF16); on_b = Buf("ones")
    S.op("pool", lambda e: e.memset(ones[:], 1.0), writes=[on_b])
    xs = [sb("fn_x%d" % i, [P, KC, 512], F32) for i in range(2)]; xs_b = [Buf("x") for _ in range(2)]
    sq = [sb("fn_sq%d" % i, [P, 512], BF16) for i in range(2)]; sq_b = [Buf("sq") for _ in range(2)]
    rstd = sb("fn_rstd", [P, 512], F32); r_b = Buf("rstd")
    yo = [sb("fn_y%d" % i, [P, KC, 512], F32) for i in range(2)]; yo_b = [Buf("y") for _ in range(2)]
    ss = ps("fn_ss", [P, 512]); ss_b = Buf("ss", psum=True)
    outs = []; n = 0
    for s in range(T // 512):
        i = s % 2; tok = slice(s * 512, (s + 1) * 512)
        S.dma("sp", xs[i][:], xin[:, :, tok], writes=[xs_b[i]])
        for k in range(KC):
            j = n % 2; n += 1
            S.op("act", lambda e, j=j, k=k: e.activation(out=sq[j][:], in_=xs[i][:, k, :], func=AF.Square), reads=[xs_b[i]], writes=[sq_b[j]])
            S.op("pe", lambda e, j=j, k=k: e.matmul(ss[:], lhsT=ones[:], rhs=sq[j][:], start=(k == 0), stop=(k == KC - 1)), reads=[on_b, sq_b[j]], writes=[ss_b])
        S.op("act", lambda e: e.activation(out=rstd[:], in_=ss[:], func=AF.Sqrt, scale=1.0 / D, bias=eps), reads=[ss_b], writes=[r_b])
        S.op("dve", lambda e: e.reciprocal(out=rstd[:], in_=rstd[:]), reads=[r_b], writes=[r_b])
        for k in range(KC):
            S.op("dve", lambda e, k=k: e.scalar_tensor_tensor(out=yo[i][:, k, :], in0=xs[i][:, k, :], scalar=g[:, k:k + 1], in1=rstd[:], op0=ALU.mult, op1=ALU.mult),
                 reads=[xs_b[i], g_b, r_b], writes=[yo_b[i]])
        ob = Buf("o"); outs.append(ob)
        S.dma("sp", yout[:, :, tok], yo[i][:], reads=[yo_b[i]], writes=[ob])
    return cms, outs

D, DFF, DEPTH = 2048, 5632, 2
WEIGHTS = [("w_ada", [DEPTH, D, 9 * D]), ("b_ada", [DEPTH, 9 * D]), ("norm_ffn1", [DEPTH, D]), ("ffn1_w13", [DEPTH, D, 2 * DFF]), ("ffn1_w2", [DEPTH, DFF, D]),
           ("norm_mix", [DEPTH, D]), ("w_in", [DEPTH, D, 18496]), ("ssm_conv_w", [DEPTH, 4, 6144]), ("ssm_conv_b", [DEPTH, 6144]), ("dt_bias", [DEPTH, 64]),
           ("a_log", [DEPTH, 64]), ("d_skip", [DEPTH, 64]), ("ssm_norm_w", [DEPTH, 4096]), ("w_ssm_out", [DEPTH, 4096, D]), ("dw_w", [DEPTH, 31, D]),
           ("dw_b", [DEPTH, D]), ("conv_ln_g", [DEPTH, D]), ("conv_ln_b", [DEPTH, D]), ("w_pw2", [DEPTH, D, D]), ("b_pw2", [DEPTH, D]), ("w_o", [DEPTH, D, D]),
           ("norm_ffn2", [DEPTH, D]), ("ffn2_w13", [DEPTH, D, 2 * DFF]), ("ffn2_w2", [DEPTH, DFF, D]), ("final_norm", [D])]


def build_program(T, depth=DEPTH, TH=2048, TT=1024, upto=None):
    nc = bass.Bass("TRN2", target_bir_lowering=False)
    dr = lambda n, s: nc.dram_tensor(n, s, F32, kind="ExternalInput").ap()
    xT = dr("xT", [D, T]); c_col = dr("c_col", [D, 1]); cst = dr("cst", [128, 640])
    W = {n: dr(n, s) for n, s in WEIGHTS}
    yT = nc.dram_tensor("yT", [D, T], F32, kind="ExternalOutput").ap()
    scr = lambda n, r: nc.dram_tensor(n, [r, T], F32, kind="Internal").ap()
    xa, xb = scr("s_xa", D), scr("s_xb", D)
    sz, xbc, dtr, u, gat, xbcs, uc, y = scr("s_sz", 4096), scr("s_xbc", 6144), scr("s_dtr", 64), scr("s_u", 2048), scr("s_gat", 4096), scr("s_xbcs", 6144), scr("s_uc", 2048), scr("s_y", 4096)
    S = Sched(nc)
    cm_mod = nc.sbuf_tensor("mod", [128, 9, 16], F32); mod = cm_mod.__enter__(); mod_b = Buf("mod")

    def phase(r):
        cms = r[0] if isinstance(r, tuple) else r
        S.phase_end()
        for cm in reversed(cms):
            cm.__exit__(None, None, None)

    TH = min(TH, T); TT = min(TT, T)
    cur = xT
    stages = 0
    for l in range(depth):
        phase(emit_ada(S, nc, c_col, W["w_ada"][l], W["b_ada"][l], [W["norm_ffn1"][l], W["norm_mix"][l], W["norm_ffn2"][l]], D, mod, mod_b))
        nxt = xa if cur is not xa else xb
        phase(emit_ffn_s(S, nc, cur, nxt, W["ffn1_w13"][l], W["ffn1_w2"][l], mod[:, 0, :], mod[:, 1, :], mod[:, 2, :], mod_b, mod_b, mod_b, T, TT, D, DFF))
        cur = nxt
        for t0 in range(0, T, TH):
            ts = slice(t0, t0 + TH)

            def side31(S_, nc_, sb_, outs_, l=l, t0=t0):
                w31, w31b, b31, b31b = load_chan_params(S_, nc_, sb_, W["dw_w"][l], W["dw_b"][l], 2048, 31, "c31")
                yield from gen_dwconv(S_, nc_, sb_, u, uc, w31, w31b, b31, b31b, 2048, 31, T, None, "c31", outs_, TB=min(1024, TH), t_start=t0, t_end=t0 + TH)
            phase(emit_mix1(S, nc, cur[:, ts], W["w_in"][l], mod[:, 3, :], mod[:, 4, :], mod_b, mod_b, sz[:, ts], xbc[:, ts], dtr[:, ts], u[:, ts], gat[:, ts], TH, D,
                            side=side31))
        phase(emit_mix2(S, nc, xbc, u, xbcs, uc, W["ssm_conv_w"][l], W["ssm_conv_b"][l], W["dw_w"][l], W["dw_b"][l], T, do31=False))
        phase(emit_mix3(S, nc, xbcs, dtr, y, cst, W["dt_bias"][l], W["a_log"][l], W["d_skip"][l], T))
        nxt = xa if cur is not xa else xb
        phase(emit_mix4(S, nc, y, sz, uc, gat, cur, nxt, W["ssm_norm_w"][l], W["w_ssm_out"][l], W["conv_ln_g"][l], W["conv_ln_b"][l], W["w_pw2"][l], W["b_pw2"][l], W["w_o"][l],
                        mod[:, 5, :], mod_b, T, D))
        cur = nxt
        nxt = xa if cur is not xa else xb
        phase(emit_ffn_s(S, nc, cur, nxt, W["ffn2_w13"][l], W["ffn2_w2"][l], mod[:, 6, :], mod[:, 7, :], mod[:, 8, :], mod_b, mod_b, mod_b, T, TT, D, DFF))
        cur = nxt
    cms, outs = emit_final(S, nc, cur, yT, W["final_norm"], T, D)
    S.drain("sp")
    S.drain("act"); S.drain("dve"); S.drain("pe"); S.drain("pool")
    return nc, S


def kernel(**inputs):
    T = 4096
    nc, _ = build_program(T)
    x = np.asarray(inputs["x"], dtype=np.float32); c = np.asarray(inputs["c"], dtype=np.float32)
    cst = make_consts()
    wts = {n: np.ascontiguousarray(np.asarray(inputs[n], dtype=np.float32)) for n, _ in WEIGHTS}
    in_maps = []
    for core in range(8):
        b = core % 4
        m = {"xT": np.ascontiguousarray(x[b].T), "c_col": np.ascontiguousarray(c[b].reshape(D, 1)), "cst": cst}
        m.update(wts)
        in_maps.append(m)
    res = run_bass_kernel_spmd(nc, in_maps, core_ids=list(range(8)))
    return np.stack([np.ascontiguousarray(res.results[b]["yT"].T) for b in range(4)]).astype(np.float32)
```

```python
import numpy as np
import concourse.bass as bass
import concourse.mybir as mybir
from concourse.bass_utils import run_bass_kernel_spmd

_uid = [0]
def _uniq(name):
    _uid[0] += 1
    return "%s_%d" % (name, _uid[0])


AF = mybir.ActivationFunctionType
ALU = mybir.AluOpType
F32 = mybir.dt.float32
BF16 = mybir.dt.bfloat16


class Buf:
    __slots__ = ("name", "w", "r", "psum", "sem", "sem_val", "sid")

    def __init__(self, name, psum=False):
        self.name = name
        self.sem = None
        self.sem_val = 0
        self.sid = None
        self.psum = psum
        self.w = None
        self.r = []


class Sched:
    ENGS = ("pe", "act", "dve", "pool", "sp")

    def __init__(self, nc, n_dma_sems=0):
        self.nc = nc
        self.eng = {"pe": nc.tensor, "act": nc.scalar, "dve": nc.vector, "pool": nc.gpsimd, "sp": nc.sync}
        self._ctx = []
        self.prog = {}
        for e in self.ENGS:
            cm = nc.semaphore("prog_" + e)
            self.prog[e] = cm.__enter__(); self._ctx.append(cm)
        self.seq = {e: 0 for e in self.ENGS}
        self.waited = {e: {} for e in self.ENGS}
        self.dnext = 0
        self.n_wait = 0
        self.owners = []
        self.sem_pool = []
        self.n_sems = 0

    def close(self):
        for cm in reversed(self._ctx):
            cm.__exit__(None, None, None)

    def _wait(self, e, ticket):
        if ticket is None:
            return
        if ticket[0] == "eng":
            key, sem, val = ("e", ticket[1]), self.prog[ticket[1]], ticket[2]
        else:
            key, sem, val = ("d", ticket[1].sid), ticket[1].sem, ticket[2]
        if self.waited[e].get(key, 0) >= val:
            return
        self.eng[e].wait_ge(sem, val)
        self.waited[e][key] = val
        self.n_wait += 1

    def _deps(self, e, reads, writes):
        for b in reads:
            self._wait(e, b.w)
            if b.psum:
                for t in b.r:
                    if not (t[0] == "eng" and t[1] == e):
                        self._wait(e, t)
        for b in writes:
            self._wait(e, b.w)
            for t in b.r:
                self._wait(e, t)

    def _commit(self, ticket, reads, writes):
        for b in reads:
            b.r.append(ticket)
            if len(b.r) > 24:
                best = {}
                for t in b.r:
                    k = (t[0], t[1] if t[0] == "eng" else t[1].sid)
                    if k not in best or best[k][2] < t[2]:
                        best[k] = t
                b.r = list(best.values())
        for b in writes:
            b.w = ticket
            b.r = []

    def op(self, e, fn, reads=(), writes=()):
        self._deps(e, reads, writes)
        ins = fn(self.eng[e])
        self.seq[e] += 1
        ins.then_inc(self.prog[e], 1)
        self._commit(("eng", e, self.seq[e]), reads, writes)
        return ins

    def mm_group(self, fns, reads=(), writes=()):
        self._deps("pe", reads, writes)
        ins = None
        for fn in fns:
            ins = fn(self.eng["pe"])
        self.seq["pe"] += 1
        ins.then_inc(self.prog["pe"], 1)
        self._commit(("eng", "pe", self.seq["pe"]), reads, writes)

    def dma(self, q, out, in_, reads=(), writes=(), **kw):
        owner = reads[0] if reads else writes[0]
        if owner.sem is None:
            if self.sem_pool:
                owner.sem, owner.sem_val, owner.sid = self.sem_pool.pop()
            else:
                cm = self.nc.semaphore("dq%d" % self.n_sems)
                owner.sem = cm.__enter__(); self._ctx.append(cm)
                owner.sem_val = 0; owner.sid = self.n_sems; self.n_sems += 1
            self.owners.append(owner)
        self._deps(q, reads, writes)
        if owner.sem_val:
            self._wait(q, ("dma", owner, owner.sem_val))
        owner.sem_val += 16
        self.eng[q].dma_start(out=out, in_=in_, **kw).then_inc(owner.sem, 16)
        self._commit(("dma", owner, owner.sem_val), reads, writes)

    def drain(self, e):
        for o in self.ENGS:
            if self.seq[o]:
                self._wait(e, ("eng", o, self.seq[o]))
        for b in self.owners:
            if b.sem_val:
                self._wait(e, ("dma", b, b.sem_val))

    def barrier(self):
        for e in self.ENGS:
            self.drain(e)

    def phase_end(self):
        self.barrier()
        for b in self.owners:
            self.sem_pool.append((b.sem, b.sem_val, b.sid))
            b.sem = None
        self.owners = []

    def finish(self, e, bufs):
        for b in bufs:
            self._wait(e, b.w)
            for t in b.r:
                self._wait(e, t)

def make_consts():
    i = np.arange(128)
    TRI = (i[:, None] <= i[None, :]); SU = (i[:, None] > i[None, :]); MASK = (i[None, :] >= i[:, None])
    return np.ascontiguousarray(np.concatenate([TRI, SU, MASK, np.ones((128, 128)), np.eye(128)], axis=1).astype(np.float32))


P = 128


def emit_ffn(S, nc, xT_in, xT_out, w13, w2, gs, sh, hg, gs_b, sh_b, hg_b, T, TT, D, DFF, eps=1e-6, NB=2):
    KC = D // P
    HC = DFF // P
    NS = TT // 512
    assert TT % 512 == 0 and T % TT == 0 and HC % NB == 0
    w13v = w13.rearrange("(kc p) n -> p kc n", p=P)
    w2v = w2.rearrange("(hc p) n -> p hc n", p=P)
    xin = xT_in.rearrange("(kc p) t -> p kc t", p=P)
    xout = xT_out.rearrange("(kc p) t -> p kc t", p=P)
    cms = []

    def sb(name, shape, dt):
        cm = nc.sbuf_tensor(_uniq(name), shape, dt); t = cm.__enter__(); cms.append(cm); return t

    def ps(name, shape, dt=F32):
        cm = nc.psum_tensor(_uniq(name), shape, dt); t = cm.__enter__(); cms.append(cm); return t

    x_sb = sb("ffn_x", [P, KC, TT], F32);   x_b = [Buf("x%d" % k) for k in range(KC)]
    h_sb = sb("ffn_h", [P, KC, TT], BF16);  h_b = [Buf("h%d" % s) for s in range(NS)]
    hid = sb("ffn_hid", [P, HC, TT], BF16); hid_b = [[Buf("hid") for _ in range(NS)] for _ in range(HC)]
    sq = [sb("ffn_sq%d" % i, [P, 512], BF16) for i in range(2)]; sq_b = [Buf("sq") for _ in range(2)]
    tmp = [sb("ffn_tmp%d" % i, [P, 512], F32) for i in range(2)]; tmp_b = [Buf("tmp") for _ in range(2)]
    rstd = sb("ffn_rstd", [P, 512], F32); rstd_b = Buf("rstd")
    ones = sb("ffn_ones", [P, P], BF16); ones_b = Buf("ones")
    wa = [sb("ffn_wa%d" % i, [P, KC, NB * P], BF16) for i in range(2)]; wa_b = [Buf("wa") for _ in range(2)]
    wg = [sb("ffn_wg%d" % i, [P, KC, NB * P], BF16) for i in range(2)]; wg_b = [Buf("wg") for _ in range(2)]
    w2s = [sb("ffn_w2%d" % i, [P, HC, P], BF16) for i in range(2)]; w2_b = [Buf("w2") for _ in range(2)]
    sg = [sb("ffn_sg%d" % i, [P, 512], F32) for i in range(2)]; sg_b = [Buf("sg") for _ in range(2)]
    xo = [sb("ffn_xo%d" % i, [P, 512], F32) for i in range(2)]; xo_b = [Buf("xo") for _ in range(2)]
    ss_ps = ps("ffn_ss", [P, 512]); ss_b = Buf("ss_ps", psum=True)
    a_ps = [ps("ffn_a%d" % i, [P, 512]) for i in range(2)]; a_b = [Buf("a_ps", psum=True) for _ in range(2)]
    g_ps = [ps("ffn_g%d" % i, [P, 512]) for i in range(2)]; g_b = [Buf("g_ps", psum=True) for _ in range(2)]
    o_ps = [ps("ffn_o%d" % i, [P, 512]) for i in range(2)]; o_b = [Buf("o_ps", psum=True) for _ in range(2)]
    out_bufs = []

    S.op("pool", lambda e: e.memset(ones[:], 1.0), writes=[ones_b])
    cnt = {"sq": 0, "ag": 0, "o": 0, "w": 0, "w2": 0}

    for tt in range(T // TT):
        t0 = tt * TT
        for k in range(KC):
            S.dma("sp", x_sb[:, k, :], xin[:, k, t0:t0 + TT], writes=[x_b[k]])
        for s in range(NS):
            sl = slice(s * 512, (s + 1) * 512)
            fns = []
            for k in range(KC):
                i = cnt["sq"] % 2; cnt["sq"] += 1
                S.op("act", lambda e, i=i, k=k: e.activation(out=sq[i][:], in_=x_sb[:, k, sl], func=AF.Square),
                     reads=[x_b[k]], writes=[sq_b[i]])
                S.op("pe", lambda e, i=i, k=k: e.matmul(ss_ps[:], lhsT=ones[:], rhs=sq[i][:], start=(k == 0), stop=(k == KC - 1)),
                     reads=[ones_b, sq_b[i]], writes=[ss_b])
            S.op("act", lambda e: e.activation(out=rstd[:], in_=ss_ps[:], func=AF.Sqrt, scale=1.0 / D, bias=eps),
                 reads=[ss_b], writes=[rstd_b])
            S.op("dve", lambda e: e.reciprocal(out=rstd[:], in_=rstd[:]), reads=[rstd_b], writes=[rstd_b])
            for k in range(KC):
                i = k % 2
                S.op("dve", lambda e, i=i, k=k: e.tensor_tensor(out=tmp[i][:], in0=x_sb[:, k, sl], in1=rstd[:], op=ALU.mult),
                     reads=[x_b[k], rstd_b], writes=[tmp_b[i]])
                S.op("act", lambda e, i=i, k=k: e.activation(out=h_sb[:, k, sl], in_=tmp[i][:], func=AF.Identity,
                                                            scale=gs[:, k:k + 1], bias=sh[:, k:k + 1]),
                     reads=[tmp_b[i], gs_b, sh_b], writes=[h_b[s]])
        for jb in range(HC // NB):
            wi = cnt["w"] % 2; cnt["w"] += 1
            c0 = jb * NB * P
            S.dma("pool", wa[wi][:], w13v[:, :, c0:c0 + NB * P], writes=[wa_b[wi]])
            S.dma("pool", wg[wi][:], w13v[:, :, DFF + c0:DFF + c0 + NB * P], writes=[wg_b[wi]])
            for jj in range(NB):
                j = jb * NB + jj
                for s in range(NS):
                    sl = slice(s * 512, (s + 1) * 512)
                    pi = cnt["ag"] % 2; cnt["ag"] += 1
                    S.mm_group([lambda e, k=k, pi=pi: e.matmul(a_ps[pi][:], lhsT=wa[wi][:, k, jj * P:(jj + 1) * P], rhs=h_sb[:, k, sl],
                                                               start=(k == 0), stop=(k == KC - 1)) for k in range(KC)],
                               reads=[wa_b[wi], h_b[s]], writes=[a_b[pi]])
                    S.mm_group([lambda e, k=k, pi=pi: e.matmul(g_ps[pi][:], lhsT=wg[wi][:, k, jj * P:(jj + 1) * P], rhs=h_sb[:, k, sl],
                                                               start=(k == 0), stop=(k == KC - 1)) for k in range(KC)],
                               reads=[wg_b[wi], h_b[s]], writes=[g_b[pi]])
                    S.op("act", lambda e, pi=pi: e.activation(out=sg[pi][:], in_=g_ps[pi][:], func=AF.Silu),
                         reads=[g_b[pi]], writes=[sg_b[pi]])
                    S.op("dve", lambda e, pi=pi, j=j: e.tensor_tensor(out=hid[:, j, sl], in0=a_ps[pi][:], in1=sg[pi][:], op=ALU.mult),
                         reads=[a_b[pi], sg_b[pi]], writes=[hid_b[j][s]])
        for o in range(KC):
            wi = cnt["w2"] % 2; cnt["w2"] += 1
            S.dma("pool", w2s[wi][:], w2v[:, :, o * P:(o + 1) * P], writes=[w2_b[wi]])
            for s in range(NS):
                sl = slice(s * 512, (s + 1) * 512)
                pi = cnt["o"] % 2; cnt["o"] += 1
                S.mm_group([lambda e, j=j, pi=pi: e.matmul(o_ps[pi][:], lhsT=w2s[wi][:, j, :], rhs=hid[:, j, sl],
                                                           start=(j == 0), stop=(j == HC - 1)) for j in range(HC)],
                           reads=[w2_b[wi]] + [hid_b[j][s] for j in range(HC)], writes=[o_b[pi]])
                S.op("dve", lambda e, pi=pi, o=o: e.scalar_tensor_tensor(out=xo[pi][:], in0=o_ps[pi][:], scalar=hg[:, o:o + 1],
                                                                          in1=x_sb[:, o, sl], op0=ALU.mult, op1=ALU.add),
                     reads=[o_b[pi], x_b[o], hg_b], writes=[xo_b[pi]])
                ob = Buf("xout"); out_bufs.append(ob)
                S.dma("sp", xout[:, o, t0 + s * 512:t0 + (s + 1) * 512], xo[pi][:], reads=[xo_b[pi]], writes=[ob])
    return cms, out_bufs


def emit_ffn_s(S, nc, xT_in, xT_out, w13, w2, gs, sh, hg, gs_b, sh_b, hg_b, T, TT, D, DFF, eps=1e-6, NB=2):
    KC = D // P; HC = DFF // P; NS = TT // 512
    assert TT % 512 == 0 and T % TT == 0 and HC % NB == 0
    w13v = w13.rearrange("(kc p) n -> p kc n", p=P); w2v = w2.rearrange("(hc p) n -> p hc n", p=P)
    xin = xT_in.rearrange("(kc p) t -> p kc t", p=P); xout = xT_out.rearrange("(kc p) t -> p kc t", p=P)
    cms = []

    def sb(name, shape, dt):
        cm = nc.sbuf_tensor(_uniq(name), shape, dt); t = cm.__enter__(); cms.append(cm); return t

    def ps(name, shape, dt=F32):
        cm = nc.psum_tensor(_uniq(name), shape, dt); t = cm.__enter__(); cms.append(cm); return t

    h_sb = sb("ffs_h", [P, KC, TT], BF16); h_b = [Buf("h%d" % s) for s in range(NS)]
    hid = sb("ffs_hid", [P, HC, TT], BF16); hid_b = [[Buf("hid") for _ in range(NS)] for _ in range(HC)]
    ld = [sb("ffs_ld%d" % i, [P, 512], F32) for i in range(4)]; ld_b = [Buf("ld") for _ in range(4)]
    sq = [sb("ffs_sq%d" % i, [P, 512], BF16) for i in range(2)]; sq_b = [Buf("sq") for _ in range(2)]
    tmp = [sb("ffs_tmp%d" % i, [P, 512], F32) for i in range(2)]; tmp_b = [Buf("tmp") for _ in range(2)]
    rstd = sb("ffs_rstd", [P, 512], F32); rstd_b = Buf("rstd")
    ones = sb("ffs_ones", [P, P], BF16); ones_b = Buf("ones")
    wa = [sb("ffs_wa%d" % i, [P, KC, NB * P], BF16) for i in range(2)]; wa_b = [Buf("wa") for _ in range(2)]
    wg = [sb("ffs_wg%d" % i, [P, KC, NB * P], BF16) for i in range(2)]; wg_b = [Buf("wg") for _ in range(2)]
    w2s = [sb("ffs_w2%d" % i, [P, HC, P], BF16) for i in range(2)]; w2_b = [Buf("w2") for _ in range(2)]
    sg = [sb("ffs_sg%d" % i, [P, 512], F32) for i in range(2)]; sg_b = [Buf("sg") for _ in range(2)]
    xo = [sb("ffs_xo%d" % i, [P, 512], F32) for i in range(2)]; xo_b = [Buf("xo") for _ in range(2)]
    ss_ps = ps("ffs_ss", [P, 512]); ss_b = Buf("ss_ps", psum=True)
    a_ps = [ps("ffs_a%d" % i, [P, 512]) for i in range(2)]; a_b = [Buf("a_ps", psum=True) for _ in range(2)]
    g_ps = [ps("ffs_g%d" % i, [P, 512]) for i in range(2)]; g_b = [Buf("g_ps", psum=True) for _ in range(2)]
    o_ps = [ps("ffs_o%d" % i, [P, 512]) for i in range(2)]; o_b = [Buf("o_ps", psum=True) for _ in range(2)]
    out_bufs = []
    S.op("pool", lambda e: e.memset(ones[:], 1.0), writes=[ones_b])
    cnt = {"sq": 0, "ag": 0, "o": 0, "w": 0, "w2": 0, "ld": 0, "t": 0}

    def load(k, tok):
        i = cnt["ld"] % 4; cnt["ld"] += 1
        S.dma("sp", ld[i][:], xin[:, k, tok], writes=[ld_b[i]])
        return i

    for tt in range(T // TT):
        t0 = tt * TT
        for s in range(NS):
            sl = slice(s * 512, (s + 1) * 512); tok = slice(t0 + s * 512, t0 + (s + 1) * 512)
            for k in range(KC):
                li = load(k, tok)
                i = cnt["sq"] % 2; cnt["sq"] += 1
                S.op("act", lambda e, i=i, li=li: e.activation(out=sq[i][:], in_=ld[li][:], func=AF.Square), reads=[ld_b[li]], writes=[sq_b[i]])
                S.op("pe", lambda e, i=i, k=k: e.matmul(ss_ps[:], lhsT=ones[:], rhs=sq[i][:], start=(k == 0), stop=(k == KC - 1)), reads=[ones_b, sq_b[i]], writes=[ss_b])
            S.op("act", lambda e: e.activation(out=rstd[:], in_=ss_ps[:], func=AF.Sqrt, scale=1.0 / D, bias=eps), reads=[ss_b], writes=[rstd_b])
            S.op("dve", lambda e: e.reciprocal(out=rstd[:], in_=rstd[:]), reads=[rstd_b], writes=[rstd_b])
            for k in range(KC):
                li = load(k, tok)
                i = cnt["t"] % 2; cnt["t"] += 1
                S.op("dve", lambda e, i=i, li=li: e.tensor_tensor(out=tmp[i][:], in0=ld[li][:], in1=rstd[:], op=ALU.mult), reads=[ld_b[li], rstd_b], writes=[tmp_b[i]])
                S.op("act", lambda e, i=i, k=k: e.activation(out=h_sb[:, k, sl], in_=tmp[i][:], func=AF.Identity, scale=gs[:, k:k + 1], bias=sh[:, k:k + 1]),
                     reads=[tmp_b[i], gs_b, sh_b], writes=[h_b[s]])
        for jb in range(HC // NB):
            wi = cnt["w"] % 2; cnt["w"] += 1
            c0 = jb * NB * P
            S.dma("pool", wa[wi][:], w13v[:, :, c0:c0 + NB * P], writes=[wa_b[wi]])
            S.dma("pool", wg[wi][:], w13v[:, :, DFF + c0:DFF + c0 + NB * P], writes=[wg_b[wi]])
            for jj in range(NB):
                j = jb * NB + jj
                for s in range(NS):
                    sl = slice(s * 512, (s + 1) * 512)
                    pi = cnt["ag"] % 2; cnt["ag"] += 1
                    S.mm_group([lambda e, k=k, pi=pi: e.matmul(a_ps[pi][:], lhsT=wa[wi][:, k, jj * P:(jj + 1) * P], rhs=h_sb[:, k, sl], start=(k == 0), stop=(k == KC - 1)) for k in range(KC)],
                               reads=[wa_b[wi], h_b[s]], writes=[a_b[pi]])
                    S.mm_group([lambda e, k=k, pi=pi: e.matmul(g_ps[pi][:], lhsT=wg[wi][:, k, jj * P:(jj + 1) * P], rhs=h_sb[:, k, sl], start=(k == 0), stop=(k == KC - 1)) for k in range(KC)],
                               reads=[wg_b[wi], h_b[s]], writes=[g_b[pi]])
                    S.op("act", lambda e, pi=pi: e.activation(out=sg[pi][:], in_=g_ps[pi][:], func=AF.Silu), reads=[g_b[pi]], writes=[sg_b[pi]])
                    S.op("dve", lambda e, pi=pi, j=j: e.tensor_tensor(out=hid[:, j, sl], in0=a_ps[pi][:], in1=sg[pi][:], op=ALU.mult), reads=[a_b[pi], sg_b[pi]], writes=[hid_b[j][s]])
        for o in range(KC):
            wi = cnt["w2"] % 2; cnt["w2"] += 1
            S.dma("pool", w2s[wi][:], w2v[:, :, o * P:(o + 1) * P], writes=[w2_b[wi]])
            for s in range(NS):
                sl = slice(s * 512, (s + 1) * 512); tok = slice(t0 + s * 512, t0 + (s + 1) * 512)
                pi = cnt["o"] % 2; cnt["o"] += 1
                S.mm_group([lambda e, j=j, pi=pi: e.matmul(o_ps[pi][:], lhsT=w2s[wi][:, j, :], rhs=hid[:, j, sl], start=(j == 0), stop=(j == HC - 1)) for j in range(HC)],
                           reads=[w2_b[wi]] + [hid_b[j][s] for j in range(HC)], writes=[o_b[pi]])
                li = load(o, tok)
                S.op("dve", lambda e, pi=pi, o=o, li=li: e.scalar_tensor_tensor(out=xo[pi][:], in0=o_ps[pi][:], scalar=hg[:, o:o + 1], in1=ld[li][:], op0=ALU.mult, op1=ALU.add),
                     reads=[o_b[pi], ld_b[li], hg_b], writes=[xo_b[pi]])
                ob = Buf("xout"); out_bufs.append(ob)
                S.dma("sp", xout[:, o, tok], xo[pi][:], reads=[xo_b[pi]], writes=[ob])
    return cms, out_bufs


P = 128


def emit_ada(S, nc, c_col, w_ada, b_ada, norms, D, out_tile, out_buf, WB=8):
    KC = D // P
    NF = 9 * KC
    wv = w_ada.rearrange("(kc p) n -> p kc n", p=P)
    cms = []

    def sb(name, shape, dt):
        cm = nc.sbuf_tensor(_uniq(name), shape, dt); t = cm.__enter__(); cms.append(cm); return t

    def ps(name, shape, dt=F32):
        cm = nc.psum_tensor(_uniq(name), shape, dt); t = cm.__enter__(); cms.append(cm); return t

    cact = sb("ada_c", [P, KC], F32); c_b = Buf("cact")
    csig = sb("ada_cs", [P, KC], F32); cs_b = Buf("csig")
    bias = sb("ada_b", [P, NF], F32); b_b = Buf("bias")
    nrm = sb("ada_n", [P, 3, KC], F32); n_b = Buf("nrm")
    raw = sb("ada_raw", [P, NF], F32); raw_b = Buf("raw")
    wt = [sb("ada_w%d" % i, [P, KC, WB * P], F32) for i in range(2)]; w_b = [Buf("w") for _ in range(2)]
    acc = ps("ada_ps", [P, NF]); acc_b = Buf("acc", psum=True)

    S.dma("sp", cact[:], c_col.rearrange("(kc p) o -> p (kc o)", p=P), writes=[c_b], allow_slow_non_contiguous=True)
    S.dma("sp", bias[:], b_ada.rearrange("(n p) -> p n", p=P), writes=[b_b], allow_slow_non_contiguous=True)
    for i, nv in enumerate(norms):
        S.dma("sp", nrm[:, i, :], nv.rearrange("(kc p) -> p kc", p=P), writes=[n_b], allow_slow_non_contiguous=True)
    S.op("act", lambda e: e.activation(out=csig[:], in_=cact[:], func=AF.Sigmoid), reads=[c_b], writes=[cs_b])
    S.op("dve", lambda e: e.tensor_tensor(out=cact[:], in0=cact[:], in1=csig[:], op=ALU.mult), reads=[c_b, cs_b], writes=[c_b])
    for fb in range(NF // WB):
        wi = fb % 2
        S.dma("sp", wt[wi][:], wv[:, :, fb * WB * P:(fb + 1) * WB * P], writes=[w_b[wi]])
        for j in range(WB):
            f = fb * WB + j
            S.mm_group([lambda e, k=k, f=f, j=j: e.matmul(acc[:, f:f + 1], lhsT=wt[wi][:, k, j * P:(j + 1) * P], rhs=cact[:, k:k + 1],
                                                          start=(k == 0), stop=(k == KC - 1)) for k in range(KC)],
                       reads=[w_b[wi], c_b], writes=[acc_b])
    S.op("dve", lambda e: e.tensor_tensor(out=raw[:], in0=acc[:], in1=bias[:], op=ALU.add), reads=[acc_b, b_b], writes=[raw_b])
    r = lambda s: raw[:, s * KC:(s + 1) * KC]
    for i in range(3):
        S.op("dve", lambda e, i=i: e.scalar_tensor_tensor(out=out_tile[:, 3 * i, :], in0=r(3 * i + 1), scalar=1.0, in1=nrm[:, i, :],
                                                         op0=ALU.add, op1=ALU.mult), reads=[raw_b, n_b], writes=[out_buf])
        S.op("dve", lambda e, i=i: e.tensor_copy(out=out_tile[:, 3 * i + 1, :], in_=r(3 * i)), reads=[raw_b], writes=[out_buf])
        gm = 1.0 if i == 1 else 0.5
        S.op("dve", lambda e, i=i, gm=gm: e.tensor_scalar(out=out_tile[:, 3 * i + 2, :], in0=r(3 * i + 2), scalar1=gm, scalar2=None, op0=ALU.mult),
             reads=[raw_b], writes=[out_buf])
    return cms


P = 128
DI, CD, NH, DC = 4096, 6144, 64, 2048
SEG_Z, SEG_X, SEG_DT, SEG_UA, SEG_UG, SEG_G = 0, DI, DI + CD, DI + CD + NH, DI + CD + NH + DC, DI + CD + NH + 2 * DC


def emit_norm_mod(S, nc, sb, ps, xin, h_sb, h_b, gs, sh, gs_b, sh_b, T, D, eps, tag):
    KC = D // P
    xs = [sb(tag + "_x%d" % i, [P, KC, 512], F32) for i in range(2)]; xs_b = [Buf("x") for _ in range(2)]
    sq = [sb(tag + "_sq%d" % i, [P, 512], BF16) for i in range(2)]; sq_b = [Buf("sq") for _ in range(2)]
    tmp = [sb(tag + "_tmp%d" % i, [P, 512], F32) for i in range(2)]; tmp_b = [Buf("tmp") for _ in range(2)]
    rstd = sb(tag + "_rstd", [P, 512], F32); rstd_b = Buf("rstd")
    ones = sb(tag + "_ones", [P, P], BF16); ones_b = Buf("ones")
    ss_ps = ps(tag + "_ss", [P, 512]); ss_b = Buf("ss")
    S.op("pool", lambda e: e.memset(ones[:], 1.0), writes=[ones_b])
    n = 0
    for s in range(T // 512):
        xi = s % 2
        S.dma("sp", xs[xi][:], xin[:, :, s * 512:(s + 1) * 512], writes=[xs_b[xi]])
        sl = slice(s * 512, (s + 1) * 512)
        for k in range(KC):
            i = n % 2; n += 1
            S.op("act", lambda e, i=i, k=k: e.activation(out=sq[i][:], in_=xs[xi][:, k, :], func=AF.Square), reads=[xs_b[xi]], writes=[sq_b[i]])
            S.op("pe", lambda e, i=i, k=k: e.matmul(ss_ps[:], lhsT=ones[:], rhs=sq[i][:], start=(k == 0), stop=(k == KC - 1)),
                 reads=[ones_b, sq_b[i]], writes=[ss_b])
        S.op("act", lambda e: e.activation(out=rstd[:], in_=ss_ps[:], func=AF.Sqrt, scale=1.0 / D, bias=eps), reads=[ss_b], writes=[rstd_b])
        S.op("dve", lambda e: e.reciprocal(out=rstd[:], in_=rstd[:]), reads=[rstd_b], writes=[rstd_b])
        for k in range(KC):
            i = k % 2
            S.op("dve", lambda e, i=i, k=k: e.tensor_tensor(out=tmp[i][:], in0=xs[xi][:, k, :], in1=rstd[:], op=ALU.mult),
                 reads=[xs_b[xi], rstd_b], writes=[tmp_b[i]])
            S.op("act", lambda e, i=i, k=k: e.activation(out=h_sb[:, k, sl], in_=tmp[i][:], func=AF.Identity, scale=gs[:, k:k + 1], bias=sh[:, k:k + 1]),
                 reads=[tmp_b[i], gs_b, sh_b], writes=[h_b[s]])


def emit_mix1(S, nc, xT_in, w_in, gs, sh, gs_b, sh_b, sz, xbc, dtr, u, gat, T, D, eps=1e-6, side=None, PUMP=3):
    KC = D // P
    NT = T // 512
    wv = w_in.rearrange("(kc p) n -> p kc n", p=P)
    xin = xT_in.rearrange("(kc p) t -> p kc t", p=P)
    cms = []

    def sb(name, shape, dt):
        cm = nc.sbuf_tensor(_uniq(name), shape, dt); t = cm.__enter__(); cms.append(cm); return t

    def ps(name, shape, dt=F32):
        cm = nc.psum_tensor(_uniq(name), shape, dt); t = cm.__enter__(); cms.append(cm); return t

    h_sb = sb("m1_h", [P, KC, T], BF16); h_b = [Buf("h") for _ in range(NT)]
    emit_norm_mod(S, nc, sb, ps, xin, h_sb, h_b, gs, sh, gs_b, sh_b, T, D, eps, "m1")

    WB = 4
    NWB = 2 if side is not None else 3
    wt = [sb("m1_w%d" % i, [P, KC, WB * P], BF16) for i in range(NWB)]; wt_b = [Buf("w") for _ in range(NWB)]
    pp = [ps("m1_p%d" % i, [P, 512]) for i in range(4)]; pp_b = [Buf("p", psum=True) for _ in range(4)]
    ev = [sb("m1_ev%d" % i, [P, 512], F32) for i in range(4)]; ev_b = [Buf("ev") for _ in range(4)]
    outs = []
    st = {"w": 0, "p": 0, "e": 0}
    side_gen = [None]

    def pump(k):
        if side_gen[0] is not None:
            for _ in range(k):
                if next(side_gen[0], "done") == "done":
                    side_gen[0] = None
                    break

    def load_w(col0, ncols):
        wi = st["w"] % NWB; st["w"] += 1
        S.dma("pool", wt[wi][:, :, 0:ncols], wv[:, :, col0:col0 + ncols], writes=[wt_b[wi]])
        return wi

    def proj(wi, c0, m, t):
        pi = st["p"] % 4; st["p"] += 1
        sl = slice(t * 512, (t + 1) * 512)
        S.mm_group([lambda e, k=k: e.matmul(pp[pi][0:m, :], lhsT=wt[wi][:, k, c0:c0 + m], rhs=h_sb[:, k, sl], start=(k == 0), stop=(k == KC - 1))
                    for k in range(KC)], reads=[wt_b[wi], h_b[t]], writes=[pp_b[pi]])
        return pi

    def store(dst, row0, m, t, ei):
        ob = Buf("o"); outs.append(ob)
        S.dma("sp", dst[row0:row0 + m, t * 512:(t + 1) * 512], ev[ei][0:m, :], reads=[ev_b[ei]], writes=[ob])

    def simple_seg(col0, width, dst, func):
        for b0 in range(0, width, WB * P):
            nb = min(WB * P, width - b0)
            wi = load_w(col0 + b0, nb)
            for c in range(0, nb, P):
                m = min(P, nb - c)
                for t in range(NT):
                    pi = proj(wi, c, m, t)
                    ei = st["e"] % 4; st["e"] += 1
                    if func is None:
                        S.op("dve", lambda e, pi=pi, ei=ei, m=m: e.tensor_copy(out=ev[ei][0:m, :], in_=pp[pi][0:m, :]), reads=[pp_b[pi]], writes=[ev_b[ei]])
                    else:
                        S.op("act", lambda e, pi=pi, ei=ei, m=m: e.activation(out=ev[ei][0:m, :], in_=pp[pi][0:m, :], func=func), reads=[pp_b[pi]], writes=[ev_b[ei]])
                    store(dst, b0 + c, m, t, ei)
                    pump(PUMP)

    for b0 in range(0, DC, WB * P):
        wa = load_w(SEG_UA + b0, WB * P)
        wg = load_w(SEG_UG + b0, WB * P)
        for c in range(0, WB * P, P):
            for t in range(NT):
                pa = proj(wa, c, P, t)
                pg = proj(wg, c, P, t)
                e1 = st["e"] % 4; st["e"] += 1
                S.op("act", lambda e, pg=pg, e1=e1: e.activation(out=ev[e1][:], in_=pp[pg][:], func=AF.Sigmoid), reads=[pp_b[pg]], writes=[ev_b[e1]])
                S.op("dve", lambda e, pa=pa, e1=e1: e.tensor_tensor(out=ev[e1][:], in0=pp[pa][:], in1=ev[e1][:], op=ALU.mult),
                     reads=[pp_b[pa], ev_b[e1]], writes=[ev_b[e1]])
                store(u, b0 + c, P, t, e1)
    if side is not None:
        S.drain("sp")
        side_gen[0] = side(S, nc, sb, outs)
    simple_seg(SEG_Z, DI, sz, AF.Silu)
    simple_seg(SEG_X, CD, xbc, None)
    simple_seg(SEG_DT, NH, dtr, None)
    simple_seg(SEG_G, 2 * DC, gat, AF.Sigmoid)
    while side_gen[0] is not None:
        pump(64)
    return cms, outs


P = 128


def gen_dwconv(S, nc, sb, src, dst, wts, wts_b, bias, bias_b, C, K, T, func, tag, outs, engines=("dve",), TB=None, t_start=0, t_end=None):
    NCH = C // P
    H = K - 1
    TB = T if TB is None else min(TB, T)
    xin = [sb(tag + "_in%d" % i, [P, H + TB], F32) for i in range(2)]; xin_b = [Buf("cin") for _ in range(2)]
    acc = [sb(tag + "_acc%d" % i, [P, TB], F32) for i in range(2)]; acc_b = [Buf("cacc") for _ in range(2)]
    if "actpool" in engines:
        tmpc = [sb(tag + "_tmp%d" % i, [P, TB], F32) for i in range(2)]; tmpc_b = [Buf("ctmp") for _ in range(2)]
    it = 0
    for c in range(NCH):
        eng = engines[c % len(engines)]
        for t0 in range(t_start, (T if t_end is None else t_end), TB):
            i = it % 2; it += 1
            if t0 == 0:
                S.op("pool", lambda e, i=i: e.memset(xin[i][:, 0:H], 0.0), writes=[xin_b[i]])
                S.dma("sp", xin[i][:, H:H + TB], src[c * P:(c + 1) * P, 0:TB], writes=[xin_b[i]])
            else:
                S.dma("sp", xin[i][:, 0:H + TB], src[c * P:(c + 1) * P, t0 - H:t0 + TB], writes=[xin_b[i]])
            yield
            if eng == "actpool":
                S.op("act", lambda e, i=i, c=c: e.activation(out=acc[i][:], in_=xin[i][:, 0:TB], func=AF.Identity, scale=wts[:, c, 0:1]),
                     reads=[xin_b[i], wts_b], writes=[acc_b[i]])
                yield
                for k in range(1, K):
                    j = k % 2
                    S.op("act", lambda e, i=i, c=c, k=k, j=j: e.activation(out=tmpc[j][:], in_=xin[i][:, k:k + TB], func=AF.Identity, scale=wts[:, c, k:k + 1]),
                         reads=[xin_b[i], wts_b], writes=[tmpc_b[j]])
                    S.op("pool", lambda e, i=i, j=j: e.tensor_tensor(out=acc[i][:], in0=acc[i][:], in1=tmpc[j][:], op=ALU.add),
                         reads=[acc_b[i], tmpc_b[j]], writes=[acc_b[i]])
                    yield
            else:
                S.op(eng, lambda e, i=i, c=c: e.tensor_scalar(out=acc[i][:], in0=xin[i][:, 0:TB], scalar1=wts[:, c, 0:1], scalar2=None, op0=ALU.mult),
                     reads=[xin_b[i], wts_b], writes=[acc_b[i]])
                yield
                for k in range(1, K):
                    S.op(eng, lambda e, i=i, c=c, k=k: e.scalar_tensor_tensor(out=acc[i][:], in0=xin[i][:, k:k + TB], scalar=wts[:, c, k:k + 1], in1=acc[i][:],
                                                                              op0=ALU.mult, op1=ALU.add),
                         reads=[xin_b[i], wts_b, acc_b[i]], writes=[acc_b[i]])
                    yield
            S.op("act", lambda e, i=i, c=c: e.activation(out=acc[i][:], in_=acc[i][:], func=(func or AF.Identity), bias=bias[:, c:c + 1]),
                 reads=[acc_b[i], bias_b], writes=[acc_b[i]])
            yield
            ob = Buf("o"); outs.append(ob)
            S.dma("sp", dst[c * P:(c + 1) * P, t0:t0 + TB], acc[i][:], reads=[acc_b[i]], writes=[ob])
            yield


def emit_dwconv(S, nc, sb, src, dst, wts, wts_b, bias, bias_b, C, K, T, func, tag, engines=("dve",), TB=None):
    outs = []
    for _ in gen_dwconv(S, nc, sb, src, dst, wts, wts_b, bias, bias_b, C, K, T, func, tag, outs, engines=engines, TB=TB):
        pass
    return outs


def load_chan_params(S, nc, sb, w_dram, b_dram, C, K, tag):
    NCH = C // P
    wt = sb(tag + "_w", [P, NCH, K], F32); wb = Buf("cw")
    bt = sb(tag + "_b", [P, NCH], F32); bb = Buf("cb")
    for k in range(K):
        S.dma("sp", wt[:, :, k], w_dram[k].rearrange("(c p) -> p c", p=P), writes=[wb], allow_slow_non_contiguous=True)
    S.dma("sp", bt[:], b_dram.rearrange("(c p) -> p c", p=P), writes=[bb], allow_slow_non_contiguous=True)
    return wt, wb, bt, bb


def emit_mix2(S, nc, xbc, u, xbcs, uc, ssm_conv_w, ssm_conv_b, dw_w, dw_b, T, do31=True):
    cms = []

    def sb(name, shape, dt):
        cm = nc.sbuf_tensor(_uniq(name), shape, dt); t = cm.__enter__(); cms.append(cm); return t

    w4, w4b, b4, b4b = load_chan_params(S, nc, sb, ssm_conv_w, ssm_conv_b, 6144, 4, "c4")
    outs = emit_dwconv(S, nc, sb, xbc, xbcs, w4, w4b, b4, b4b, 6144, 4, T, AF.Silu, "c4")
    if do31:
        w31, w31b, b31, b31b = load_chan_params(S, nc, sb, dw_w, dw_b, 2048, 31, "c31")
        outs += emit_dwconv(S, nc, sb, u, uc, w31, w31b, b31, b31b, 2048, 31, T, None, "c31", engines=("dve",))
    return cms, outs


P = 128
NH, HD, NG, NS_, DI = 64, 64, 8, 128, 4096
HG = NH // NG


def emit_mix3(S, nc, xbcs, dtr, y_out, consts, dt_bias, a_log, d_skip, T, stop_after=None, dbg=None, side=None, PUMP=1):
    NCH = T // P
    cms = []

    def sb(name, shape, dt):
        cm = nc.sbuf_tensor(_uniq(name), shape, dt); t = cm.__enter__(); cms.append(cm); return t

    def ps(name, shape, dt=F32):
        cm = nc.psum_tensor(_uniq(name), shape, dt); t = cm.__enter__(); cms.append(cm); return t

    cst = sb("s3_c", [P, 5 * P], F32); cst_b = Buf("cst")
    S.dma("sp", cst[:], consts[:, :], writes=[cst_b])
    TRI, SU, MASK, ONES, IDENT = [cst[:, i * P:(i + 1) * P] for i in range(5)]
    hp = sb("s3_hp", [NH, 4], F32); hp_b = Buf("hp")
    S.dma("sp", hp[:, 0:1], dt_bias.rearrange("(h o) -> h o", o=1), writes=[hp_b])
    S.dma("sp", hp[:, 1:2], a_log.rearrange("(h o) -> h o", o=1), writes=[hp_b])
    S.op("act", lambda e: e.activation(out=hp[:, 2:3], in_=hp[:, 1:2], func=AF.Exp), reads=[hp_b], writes=[hp_b])
    S.op("dve", lambda e: e.tensor_scalar(out=hp[:, 1:2], in0=hp[:, 2:3], scalar1=-1.0, scalar2=None, op0=ALU.mult), reads=[hp_b], writes=[hp_b])
    dsk = sb("s3_dsk", [1, NH], F32); dsk_b = Buf("dsk")
    S.dma("sp", dsk[:], d_skip.rearrange("(o h) -> o h", o=1), writes=[dsk_b])
    Drow = sb("s3_Drow", [P, DI], F32); Drow_b = Buf("Drow")
    one1 = sb("s3_one1", [1, P], F32); one1_b = Buf("one1")
    S.op("pool", lambda e: e.memset(one1[:], 1.0), writes=[one1_b])
    dskb = sb("s3_dskb", [P, NH], F32); dskb_b = Buf("dskb")
    pm = ps("s3_pm", [P, 512]); pm_b = Buf("pm", psum=True)
    S.op("pe", lambda e: e.matmul(pm[:, 0:NH], lhsT=one1[:], rhs=dsk[:], start=True, stop=True), reads=[one1_b, dsk_b], writes=[pm_b])
    S.op("dve", lambda e: e.tensor_copy(out=dskb[:], in_=pm[:, 0:NH]), reads=[pm_b], writes=[dskb_b])
    for h in range(NH):
        S.op("act", lambda e, h=h: e.activation(out=Drow[:, h * HD:(h + 1) * HD], in_=cst[:, 0:HD], func=AF.Identity, scale=0.0, bias=dskb[:, h:h + 1]),
             reads=[cst_b, dskb_b], writes=[Drow_b])

    xsT = [sb("s3_xsT%d" % i, [P, 4, P], F32) for i in range(2)]; xsT_b = [Buf("xsT") for _ in range(2)]
    bcT = sb("s3_bcT", [P, 2 * NG, P], F32); bcT_b = Buf("bcT")
    Bt = sb("s3_Bt", [P, NG, P], BF16); Bt_b = Buf("Bt"); Ct = sb("s3_Ct", [P, NG, P], BF16); Ct_b = Buf("Ct")
    Btm = sb("s3_Btm", [P, NG, P], BF16); Btm_b = Buf("Btm")
    dtf = sb("s3_dtf", [NH, 2, P], F32); dtf_b = Buf("dtf")
    dtm = sb("s3_dtm", [P, 2 * NH], F32); dtm_b = Buf("dtm")
    dec = sb("s3_dec", [P, 3 * NH], F32); dec_b = Buf("dec")
    acs = sb("s3_acs", [P, 2 * NH], F32); acs_b = Buf("acs")
    xs_tm = sb("s3_xs", [P, DI], F32); xs_b = Buf("xs_tm")
    xsq_b = [Buf("xsq") for _ in range(NG)]; xdtq_b = [Buf("xdtq") for _ in range(NG)]; xdteq_b = [Buf("xdteq") for _ in range(NG)]; yq_b = [Buf("yq") for _ in range(NG)]
    xdt = sb("s3_xdt", [P, DI], BF16); xdt_b = Buf("xdt")
    xdte = sb("s3_xdte", [P, DI], BF16); xdte_b = Buf("xdte")
    y_tm = sb("s3_y", [P, DI], F32); y_b = Buf("y_tm")
    S32 = sb("s3_S32", [P, DI], F32); S32_b = Buf("S32")
    Sbf = sb("s3_Sbf", [P, DI], BF16); Sbf_b = Buf("Sbf")
    L8 = [sb("s3_L%d" % i, [P, 4, P], F32) for i in range(2)]; L8_b = [Buf("L") for _ in range(2)]
    dte = sb("s3_dte", [P, NH], F32); dte_b = Buf("dte")
    mskb = sb("s3_mskb", [P, 2 * P], BF16); mskb_b = Buf("mskb")
    S.op("act", lambda e: e.activation(out=mskb[:], in_=cst[:, 0:2 * P], func=AF.Identity), reads=[cst_b], writes=[mskb_b])
    TRIb, SUb = mskb[:, 0:P], mskb[:, P:2 * P]
    dhl = sb("s3_dhl", [P, 2 * NH], BF16); dhl_b = Buf("dhl")
    R8h = [sb("s3_Rh%d" % i, [P, 4, P], BF16) for i in range(2)]; R8l = [sb("s3_Rl%d" % i, [P, 4, P], BF16) for i in range(2)]
    CBm2 = [sb("s3_CBm%d" % i, [P, P], F32) for i in range(2)]; CBm2_b = [Buf("CBm") for _ in range(2)]
    Ex = [sb("s3_Ex%d" % i, [P, 512], F32) for i in range(2)]; Ex_b = [Buf("Ex") for _ in range(2)]
    MT2 = [sb("s3_MT%d" % i, [P, HG, P], BF16) for i in range(2)]; MT2_b = [Buf("MT") for _ in range(2)]
    yev = [sb("s3_yev%d" % i, [P, 512], F32) for i in range(2)]; yev_b = [Buf("yev") for _ in range(2)]
    tr = [ps("s3_tr%d" % i, [P, 512]) for i in range(2)]; tr_b = [Buf("tr", psum=True) for _ in range(2)]
    sg = [ps("s3_sg%d" % i, [P, 512]) for i in range(2)]; sg_b = [Buf("sg", psum=True) for _ in range(2)]
    yd = ps("s3_yd", [P, 512]); yd_b = Buf("yd", psum=True)
    yo = ps("s3_yo", [P, 512]); yo_b = Buf("yo", psum=True)
    stp = ps("s3_st", [P, 512]); stp_b = Buf("stp", psum=True)
    yd2 = [yd, tr[0]]; yd2_b = [yd_b, tr_b[0]]
    yo2 = [yo, tr[1]]; yo2_b = [yo_b, tr_b[1]]
    outs = []
    side_gen = side(S, nc, sb, outs) if side is not None else None

    def pump(k):
        if side_gen is not None:
            for _ in range(k):
                if next(side_gen, "done") == "done":
                    break

    def dv(fn, reads=(), writes=()):
        S.op("dve", fn, reads=reads, writes=writes)
        pump(PUMP)
    S.op("pool", lambda e: e.memset(S32[:], 0.0), writes=[S32_b])
    S.op("pool", lambda e: e.memset(Sbf[:], 0.0), writes=[Sbf_b])
    n = {"tr": 0, "sg": 0, "L": 0, "Ex": 0, "yev": 0, "x": 0}
    yout = y_out.rearrange("(c p) t -> p c t", p=P)

    for c in range(NCH):
        tok = slice(c * P, (c + 1) * P)
        S.dma("sp", dtf[:, 0, :], dtr[:, tok], writes=[dtf_b])
        S.op("act", lambda e: e.activation(out=dtf[:, 0, :], in_=dtf[:, 0, :], func=AF.Exp, bias=hp[:, 0:1]), reads=[dtf_b, hp_b], writes=[dtf_b])
        S.op("act", lambda e: e.activation(out=dtf[:, 0, :], in_=dtf[:, 0, :], func=AF.Ln, bias=1.0), reads=[dtf_b], writes=[dtf_b])
        dv(lambda e: e.tensor_scalar(out=dtf[:, 1, :], in0=dtf[:, 0, :], scalar1=hp[:, 1:2], scalar2=None, op0=ALU.mult), reads=[dtf_b, hp_b], writes=[dtf_b])
        ti = n["tr"] % 2; n["tr"] += 1
        S.op("pe", lambda e: e.transpose(tr[ti][:, 0:NH], dtf[:, 0, :], IDENT[0:NH, 0:NH]), reads=[dtf_b, cst_b], writes=[tr_b[ti]])
        S.op("pe", lambda e: e.transpose(tr[ti][:, NH:2 * NH], dtf[:, 1, :], IDENT[0:NH, 0:NH]), reads=[dtf_b, cst_b, tr_b[ti]], writes=[tr_b[ti]])
        dv(lambda e: e.tensor_copy(out=dtm[:], in_=tr[ti][:, 0:2 * NH]), reads=[tr_b[ti]], writes=[dtm_b])
        S.op("act", lambda e: e.activation(out=dhl[:, 0:NH], in_=dtm[:, NH:2 * NH], func=AF.Identity), reads=[dtm_b], writes=[dhl_b])
        dv(lambda e: e.tensor_tensor(out=dhl[:, NH:2 * NH], in0=dtm[:, NH:2 * NH], in1=dhl[:, 0:NH], op=ALU.subtract), reads=[dtm_b, dhl_b], writes=[dhl_b])
        S.op("pe", lambda e: e.matmul(pm[:, 0:NH], lhsT=TRI, rhs=dtm[:, NH:2 * NH], start=True, stop=True), reads=[cst_b, dtm_b], writes=[pm_b])
        S.op("pe", lambda e: e.matmul(pm[:, NH:2 * NH], lhsT=ONES, rhs=dtm[:, NH:2 * NH], start=True, stop=True), reads=[cst_b, dtm_b, pm_b], writes=[pm_b])
        dv(lambda e: e.tensor_copy(out=acs[:], in_=pm[:, 0:2 * NH]), reads=[pm_b], writes=[acs_b])
        S.op("act", lambda e: e.activation(out=dec[:, 0:NH], in_=acs[:, 0:NH], func=AF.Exp), reads=[acs_b], writes=[dec_b])
        dv(lambda e: e.tensor_tensor(out=acs[:, 0:NH], in0=acs[:, NH:2 * NH], in1=acs[:, 0:NH], op=ALU.subtract), reads=[acs_b, dec_b], writes=[acs_b])
        S.op("act", lambda e: e.activation(out=dec[:, NH:2 * NH], in_=acs[:, 0:NH], func=AF.Exp), reads=[acs_b], writes=[dec_b])
        S.op("act", lambda e: e.activation(out=dec[:, 2 * NH:3 * NH], in_=acs[:, NH:2 * NH], func=AF.Exp), reads=[acs_b], writes=[dec_b])
        def cut():
            S.drain("sp")
            ob = Buf("o"); outs.append(ob)
            S.dma("sp", dbg[:, c * 3 * NH:(c + 1) * 3 * NH], dec[:], reads=[dec_b], writes=[ob])
        if stop_after == "dec":
            cut(); continue
        S.dma("sp", bcT[:], xbcs[DI:DI + 2 * NG * NS_, tok].rearrange("(g p) t -> p g t", p=P), writes=[bcT_b])
        S.op("act", lambda e: e.activation(out=Bt[:], in_=bcT[:, 0:NG, :], func=AF.Identity), reads=[bcT_b], writes=[Bt_b])
        S.op("act", lambda e: e.activation(out=Ct[:], in_=bcT[:, NG:2 * NG, :], func=AF.Identity), reads=[bcT_b], writes=[Ct_b])
        for g4 in range(NG // 4):
            ti = n["tr"] % 2; n["tr"] += 1
            for q in range(4):
                S.op("pe", lambda e, q=q: e.transpose(tr[ti][:, q * P:(q + 1) * P], bcT[:, g4 * 4 + q, :], IDENT), reads=[bcT_b, cst_b, tr_b[ti]], writes=[tr_b[ti]])
            dv(lambda e: e.tensor_copy(out=Btm[:, g4 * 4:(g4 + 1) * 4, :], in_=tr[ti][:].rearrange("p (g n) -> p g n", g=4)), reads=[tr_b[ti]], writes=[Btm_b])
        if stop_after == "bc":
            cut(); continue
        dv(lambda e: e.tensor_tensor(out=dte[:], in0=dtm[:, 0:NH], in1=dec[:, NH:2 * NH], op=ALU.mult), reads=[dtm_b, dec_b], writes=[dte_b])
        h3 = lambda ap: ap.rearrange("p (h d) -> p h d", d=HD)
        bc = lambda ap, nh: ap.unsqueeze(2).to_broadcast([P, nh, HD])
        for f4 in range(DI // 512):
            xi = n["x"] % 2; n["x"] += 1
            S.dma("sp", xsT[xi][:], xbcs[f4 * 512:(f4 + 1) * 512, tok].rearrange("(q p) t -> p q t", p=P), writes=[xsT_b[xi]])
            ti = n["tr"] % 2; n["tr"] += 1
            for q in range(4):
                S.op("pe", lambda e, q=q: e.transpose(tr[ti][:, q * P:(q + 1) * P], xsT[xi][:, q, :], IDENT), reads=[xsT_b[xi], cst_b, tr_b[ti]], writes=[tr_b[ti]])
            fs = slice(f4 * 512, (f4 + 1) * 512); hs = slice(f4 * 8, (f4 + 1) * 8)
            S.op("act", lambda e: e.activation(out=xs_tm[:, fs], in_=tr[ti][:], func=AF.Identity), reads=[tr_b[ti]], writes=[xsq_b[f4]])
            dv(lambda e: e.tensor_tensor(out=h3(xdt[:, fs]), in0=h3(xs_tm[:, fs]), in1=bc(dtm[:, hs], 8), op=ALU.mult), reads=[xsq_b[f4], dtm_b], writes=[xdtq_b[f4]])
            dv(lambda e: e.tensor_tensor(out=h3(xdte[:, fs]), in0=h3(xs_tm[:, fs]), in1=bc(dte[:, hs], 8), op=ALU.mult), reads=[xsq_b[f4], dte_b], writes=[xdteq_b[f4]])
        if stop_after == "xs":
            cut(); continue
        def stage_A(g):
            gc = slice(g * 512, (g + 1) * 512); pb = g % 2
            CBm, CBm_b, MT, MT_b, yo_, yo_b_ = CBm2[pb], CBm2_b[pb], MT2[pb], MT2_b[pb], yo2[pb], yo2_b[pb]
            S.op("pe", lambda e: e.matmul(pm[:, 2 * NH:2 * NH + P], lhsT=Bt[:, g, :], rhs=Ct[:, g, :], start=True, stop=True), reads=[Bt_b, Ct_b, pm_b], writes=[pm_b])
            dv(lambda e: e.tensor_tensor(out=CBm[:], in0=pm[:, 2 * NH:2 * NH + P], in1=MASK, op=ALU.mult), reads=[pm_b, cst_b], writes=[CBm_b])
            S.op("pe", lambda e: e.matmul(yo_[:], lhsT=Ct[:, g, :], rhs=Sbf[:, gc], start=True, stop=True), reads=[Ct_b, Sbf_b], writes=[yo_b_])
            for h4 in range(2):
                si = n["sg"] % 2; n["sg"] += 1
                h0 = g * HG + h4 * 4
                li = n["L"] % 2; n["L"] += 1
                dv(lambda e, li=li, h0=h0: e.tensor_tensor(out=R8h[li][:], in0=TRIb.unsqueeze(1).to_broadcast([P, 4, P]),
                                                                    in1=dhl[:, h0:h0 + 4].unsqueeze(2).to_broadcast([P, 4, P]), op=ALU.mult),
                     reads=[mskb_b, dhl_b], writes=[L8_b[li]])
                dv(lambda e, li=li, h0=h0: e.tensor_tensor(out=R8l[li][:], in0=TRIb.unsqueeze(1).to_broadcast([P, 4, P]),
                                                                    in1=dhl[:, NH + h0:NH + h0 + 4].unsqueeze(2).to_broadcast([P, 4, P]), op=ALU.mult),
                     reads=[mskb_b, dhl_b, L8_b[li]], writes=[L8_b[li]])
                S.mm_group([lambda e, li=li: e.matmul(sg[si][:], lhsT=SUb, rhs=R8h[li][:].rearrange("p h l -> p (h l)"), start=True, stop=False),
                            lambda e, li=li: e.matmul(sg[si][:], lhsT=SUb, rhs=R8l[li][:].rearrange("p h l -> p (h l)"), start=False, stop=True)],
                           reads=[L8_b[li], mskb_b], writes=[sg_b[si]])
                ei = n["Ex"] % 2; n["Ex"] += 1
                S.op("act", lambda e, ei=ei: e.activation(out=Ex[ei][:], in_=sg[si][:], func=AF.Exp), reads=[sg_b[si]], writes=[Ex_b[ei]])
                dv(lambda e, ei=ei: e.tensor_tensor(out=MT[:, h4 * 4:(h4 + 1) * 4, :], in0=Ex[ei][:].rearrange("p (q l) -> p q l", q=4),
                                                              in1=CBm[:].unsqueeze(1).to_broadcast([P, 4, P]), op=ALU.mult), reads=[Ex_b[ei], CBm_b], writes=[MT_b])

        def stage_B(g):
            gc = slice(g * 512, (g + 1) * 512); pb = g % 2
            MT, MT_b, yo_, yo_b_, yd_, yd_b_ = MT2[pb], MT2_b[pb], yo2[pb], yo2_b[pb], yd2[pb], yd2_b[pb]
            for hh in range(HG):
                h = g * HG + hh
                S.op("pe", lambda e, h=h, hh=hh: e.matmul(yd_[:, hh * HD:(hh + 1) * HD], lhsT=MT[:, hh, :], rhs=xdt[:, h * HD:(h + 1) * HD], start=True, stop=True),
                     reads=[MT_b, xdtq_b[g], yd_b_], writes=[yd_b_])
            dv(lambda e: e.tensor_tensor(out=h3(y_tm[:, gc]), in0=h3(yo_[:]), in1=bc(dec[:, g * HG:(g + 1) * HG], HG), op=ALU.mult), reads=[yo_b_, dec_b], writes=[yq_b[g]])
            dv(lambda e: e.tensor_tensor(out=y_tm[:, gc], in0=yd_[:], in1=y_tm[:, gc], op=ALU.add), reads=[yd_b_, yq_b[g]], writes=[yq_b[g]])
            dv(lambda e: e.tensor_tensor(out=xs_tm[:, gc], in0=xs_tm[:, gc], in1=Drow[:, gc], op=ALU.mult), reads=[xsq_b[g], Drow_b], writes=[xsq_b[g]])
            dv(lambda e: e.tensor_tensor(out=y_tm[:, gc], in0=y_tm[:, gc], in1=xs_tm[:, gc], op=ALU.add), reads=[yq_b[g], xsq_b[g]], writes=[yq_b[g]])
            S.op("pe", lambda e: e.matmul(stp[:], lhsT=Btm[:, g, :], rhs=xdte[:, gc], start=True, stop=True), reads=[Btm_b, xdteq_b[g]], writes=[stp_b])
            dv(lambda e: e.tensor_tensor(out=h3(S32[:, gc]), in0=h3(S32[:, gc]), in1=bc(dec[:, 2 * NH + g * HG:2 * NH + (g + 1) * HG], HG), op=ALU.mult),
                 reads=[S32_b, dec_b, yo_b_], writes=[S32_b])
            dv(lambda e: e.tensor_tensor(out=S32[:, gc], in0=stp[:], in1=S32[:, gc], op=ALU.add), reads=[stp_b, S32_b], writes=[S32_b])
            S.op("act", lambda e: e.activation(out=Sbf[:, gc], in_=S32[:, gc], func=AF.Identity), reads=[S32_b, yo_b_], writes=[Sbf_b])

        stage_A(0)
        for g in range(NG):
            if g + 1 < NG:
                stage_A(g + 1)
            stage_B(g)
        if stop_after == "grp":
            cut(); continue
        for f4 in range(DI // 512):
            ti = n["tr"] % 2; n["tr"] += 1
            for q in range(4):
                fc = f4 * 4 + q
                S.op("pe", lambda e, q=q, fc=fc: e.transpose(tr[ti][:, q * P:(q + 1) * P], y_tm[:, fc * P:(fc + 1) * P], IDENT), reads=[yq_b[f4], cst_b, tr_b[ti]], writes=[tr_b[ti]])
            yi = n["yev"] % 2; n["yev"] += 1
            S.op("act", lambda e, yi=yi: e.activation(out=yev[yi][:], in_=tr[ti][:], func=AF.Identity), reads=[tr_b[ti]], writes=[yev_b[yi]])
            ob = Buf("o"); outs.append(ob)
            S.dma("sp", yout[:, f4 * 4:(f4 + 1) * 4, tok], yev[yi][:].rearrange("p (q t) -> p q t", q=4), reads=[yev_b[yi]], writes=[ob])
    if side_gen is not None:
        for _ in side_gen:
            pass
    return cms, outs


P = 128
DI, DC = 4096, 2048


def emit_mix4(S, nc, y, sz, uc, gat, xT_in, xT_out, ssm_norm_w, w_ssm_out, ln_g, ln_b, w_pw2, b_pw2, w_o, g2, g2_b, T, D, eps=1e-6, dbg=None):
    KI = DI // P
    KC = D // P
    cms = []

    def sb(name, shape, dt):
        cm = nc.sbuf_tensor(_uniq(name), shape, dt); t = cm.__enter__(); cms.append(cm); return t

    def ps(name, shape, dt=F32):
        cm = nc.psum_tensor(_uniq(name), shape, dt); t = cm.__enter__(); cms.append(cm); return t

    yv = y.rearrange("(k p) t -> p k t", p=P); szv = sz.rearrange("(k p) t -> p k t", p=P)
    ucv = uc.rearrange("(k p) t -> p k t", p=P); gv = gat.rearrange("(k p) t -> p k t", p=P)
    xin = xT_in.rearrange("(k p) t -> p k t", p=P); xout = xT_out.rearrange("(k p) t -> p k t", p=P)
    wsv = w_ssm_out.rearrange("(k p) n -> p k n", p=P); wpv = w_pw2.rearrange("(k p) n -> p k n", p=P); wov = w_o.rearrange("(k p) n -> p k n", p=P)

    prm = sb("s4_prm", [P, KI + 3 * KC], F32); prm_b = Buf("prm")
    S.dma("sp", prm[:, 0:KI], ssm_norm_w.rearrange("(k p) -> p k", p=P), writes=[prm_b], allow_slow_non_contiguous=True)
    for i, v in enumerate([ln_g, ln_b, b_pw2]):
        S.dma("sp", prm[:, KI + i * KC:KI + (i + 1) * KC], v.rearrange("(k p) -> p k", p=P), writes=[prm_b], allow_slow_non_contiguous=True)
    NW = lambda k: prm[:, k:k + 1]
    LG = lambda k: prm[:, KI + k:KI + k + 1]
    LB = lambda k: prm[:, KI + KC + k:KI + KC + k + 1]
    BP = lambda k: prm[:, KI + 2 * KC + k:KI + 2 * KC + k + 1]

    ones_b16 = sb("s4_ones", [P, P], BF16); ones_f32 = sb("s4_onesf", [P, P], F32); on_b = Buf("ones")
    S.op("pool", lambda e: e.memset(ones_b16[:], 1.0), writes=[on_b])
    S.op("pool", lambda e: e.memset(ones_f32[:], 1.0), writes=[on_b])
    ygw = sb("s4_ygw", [P, KI, 512], BF16); ygw_b = Buf("ygw")
    ucs = sb("s4_uc", [P, KC, 512], F32); ucs_b = Buf("ucs")
    ua = sb("s4_ua", [P, KC, 512], BF16); ua_b = Buf("ua")
    mt = sb("s4_m", [P, KC, 512], BF16); m_b = [Buf("m%d" % k) for k in range(KC)]
    ld = [sb("s4_ld%d" % i, [P, 512], F32) for i in range(4)]; ld_b = [Buf("ld") for _ in range(4)]
    t1 = [sb("s4_t%d" % i, [P, 512], F32) for i in range(2)]; t1_b = [Buf("t1") for _ in range(2)]
    sq = [sb("s4_sq%d" % i, [P, 512], BF16) for i in range(2)]; sq_b = [Buf("sq") for _ in range(2)]
    rs_s = sb("s4_rs", [P, 512], F32); rs_b = Buf("rs")
    mu = sb("s4_mu", [P, 512], F32); mu_b = Buf("mu")
    rc = sb("s4_rc", [P, 512], F32); rc_b = Buf("rc")
    ev = [sb("s4_ev%d" % i, [P, 512], F32) for i in range(2)]; ev_b = [Buf("ev") for _ in range(2)]
    xo = [sb("s4_xo%d" % i, [P, 512], F32) for i in range(2)]; xo_b = [Buf("xo") for _ in range(2)]
    ws = [sb("s4_ws%d" % i, [P, KI, P], BF16) for i in range(2)]; ws_b = [Buf("ws") for _ in range(2)]
    wp = [sb("s4_wp%d" % i, [P, KC, P], BF16) for i in range(2)]; wp_b = [Buf("wp") for _ in range(2)]
    wo = [sb("s4_wo%d" % i, [P, KC, P], BF16) for i in range(2)]; wo_b = [Buf("wo") for _ in range(2)]
    st_y = ps("s4_sty", [P, 512]); sty_b = Buf("sty", psum=True)
    st_u = ps("s4_stu", [P, 512]); stu_b = Buf("stu", psum=True)
    st_u2 = ps("s4_stu2", [P, 512]); stu2_b = Buf("stu2", psum=True)
    pa = [ps("s4_pa%d" % i, [P, 512]) for i in range(2)]; pa_b = [Buf("pa", psum=True) for _ in range(2)]
    po = [ps("s4_po%d" % i, [P, 512]) for i in range(2)]; po_b = [Buf("po", psum=True) for _ in range(2)]
    outs = []
    n = {"ld": 0, "t": 0, "sq": 0, "ev": 0, "xo": 0, "pa": 0, "po": 0, "ws": 0, "wp": 0, "wo": 0}

    def load(src, k, tok):
        i = n["ld"] % 4; n["ld"] += 1
        S.dma("sp", ld[i][:], src[:, k, tok], writes=[ld_b[i]])
        return i

    for tt in range(T // 512):
        tok = slice(tt * 512, (tt + 1) * 512)
        for k in range(KI):
            iy = load(yv, k, tok); iz = load(szv, k, tok)
            ti = n["t"] % 2; n["t"] += 1
            S.op("dve", lambda e, ti=ti, iy=iy, iz=iz: e.tensor_tensor(out=t1[ti][:], in0=ld[iy][:], in1=ld[iz][:], op=ALU.mult),
                 reads=[ld_b[iy], ld_b[iz]], writes=[t1_b[ti]])
            si = n["sq"] % 2; n["sq"] += 1
            S.op("act", lambda e, ti=ti, si=si: e.activation(out=sq[si][:], in_=t1[ti][:], func=AF.Square), reads=[t1_b[ti]], writes=[sq_b[si]])
            S.op("pe", lambda e, si=si, k=k: e.matmul(st_y[:], lhsT=ones_b16[:], rhs=sq[si][:], start=(k == 0), stop=(k == KI - 1)),
                 reads=[on_b, sq_b[si]], writes=[sty_b])
            S.op("dve", lambda e, ti=ti, k=k: e.tensor_scalar(out=ygw[:, k, :], in0=t1[ti][:], scalar1=NW(k), scalar2=None, op0=ALU.mult),
                 reads=[t1_b[ti], prm_b], writes=[ygw_b])
        S.op("act", lambda e: e.activation(out=rs_s[:], in_=st_y[:], func=AF.Sqrt, scale=1.0 / DI, bias=eps), reads=[sty_b], writes=[rs_b])
        S.op("dve", lambda e: e.reciprocal(out=rs_s[:], in_=rs_s[:]), reads=[rs_b], writes=[rs_b])
        S.dma("sp", ucs[:], ucv[:, :, tok], writes=[ucs_b])
        for k in range(KC):
            si = n["sq"] % 2; n["sq"] += 1
            S.op("act", lambda e, si=si, k=k: e.activation(out=sq[si][:], in_=ucs[:, k, :], func=AF.Square), reads=[ucs_b], writes=[sq_b[si]])
            S.op("pe", lambda e, si=si, k=k: e.matmul(st_u2[:], lhsT=ones_b16[:], rhs=sq[si][:], start=(k == 0), stop=(k == KC - 1)),
                 reads=[on_b, sq_b[si]], writes=[stu2_b])
            S.op("pe", lambda e, k=k: e.matmul(st_u[:], lhsT=ones_f32[:], rhs=ucs[:, k, :], start=(k == 0), stop=(k == KC - 1)),
                 reads=[on_b, ucs_b], writes=[stu_b])
        S.op("dve", lambda e: e.tensor_scalar(out=mu[:], in0=st_u[:], scalar1=1.0 / DC, scalar2=None, op0=ALU.mult), reads=[stu_b], writes=[mu_b])
        S.op("dve", lambda e: e.tensor_tensor(out=rc[:], in0=mu[:], in1=mu[:], op=ALU.mult), reads=[mu_b], writes=[rc_b])
        S.op("dve", lambda e: e.scalar_tensor_tensor(out=rc[:], in0=st_u2[:], scalar=1.0 / DC, in1=rc[:], op0=ALU.mult, op1=ALU.subtract),
             reads=[stu2_b, rc_b], writes=[rc_b])
        S.op("act", lambda e: e.activation(out=rc[:], in_=rc[:], func=AF.Sqrt, bias=eps), reads=[rc_b], writes=[rc_b])
        S.op("dve", lambda e: e.reciprocal(out=rc[:], in_=rc[:]), reads=[rc_b], writes=[rc_b])
        for k in range(KC):
            ti = n["t"] % 2; n["t"] += 1
            S.op("dve", lambda e, ti=ti, k=k: e.tensor_tensor(out=t1[ti][:], in0=ucs[:, k, :], in1=mu[:], op=ALU.subtract), reads=[ucs_b, mu_b], writes=[t1_b[ti]])
            S.op("dve", lambda e, ti=ti: e.tensor_tensor(out=t1[ti][:], in0=t1[ti][:], in1=rc[:], op=ALU.mult), reads=[t1_b[ti], rc_b], writes=[t1_b[ti]])
            S.op("act", lambda e, ti=ti, k=k: e.activation(out=ua[:, k, :], in_=t1[ti][:], func=AF.Silu, scale=LG(k), bias=LB(k)),
                 reads=[t1_b[ti], prm_b], writes=[ua_b])
        for o in range(KC):
            oc = slice(o * P, (o + 1) * P)
            wi = n["ws"] % 2; n["ws"] += 1
            S.dma("pool", ws[wi][:], wsv[:, :, oc], writes=[ws_b[wi]])
            wj = n["wp"] % 2; n["wp"] += 1
            S.dma("pool", wp[wj][:], wpv[:, :, oc], writes=[wp_b[wj]])
            p1 = n["pa"] % 2; n["pa"] += 1
            S.mm_group([lambda e, k=k: e.matmul(pa[p1][:], lhsT=ws[wi][:, k, :], rhs=ygw[:, k, :], start=(k == 0), stop=(k == KI - 1)) for k in range(KI)],
                       reads=[ws_b[wi], ygw_b], writes=[pa_b[p1]])
            p2 = n["pa"] % 2; n["pa"] += 1
            S.mm_group([lambda e, k=k: e.matmul(pa[p2][:], lhsT=wp[wj][:, k, :], rhs=ua[:, k, :], start=(k == 0), stop=(k == KC - 1)) for k in range(KC)],
                       reads=[wp_b[wj], ua_b], writes=[pa_b[p2]])
            igs = load(gv, o, tok); igc = load(gv, KC + o, tok)
            e1 = n["ev"] % 2; n["ev"] += 1
            S.op("dve", lambda e, e1=e1, p1=p1: e.tensor_tensor(out=ev[e1][:], in0=pa[p1][:], in1=rs_s[:], op=ALU.mult), reads=[pa_b[p1], rs_b], writes=[ev_b[e1]])
            S.op("dve", lambda e, e1=e1, igs=igs: e.tensor_tensor(out=ev[e1][:], in0=ev[e1][:], in1=ld[igs][:], op=ALU.mult), reads=[ev_b[e1], ld_b[igs]], writes=[ev_b[e1]])
            e2 = n["ev"] % 2; n["ev"] += 1
            S.op("act", lambda e, e2=e2, p2=p2, o=o: e.activation(out=ev[e2][:], in_=pa[p2][:], func=AF.Identity, bias=BP(o)), reads=[pa_b[p2], prm_b], writes=[ev_b[e2]])
            S.op("dve", lambda e, e2=e2, igc=igc: e.tensor_tensor(out=ev[e2][:], in0=ev[e2][:], in1=ld[igc][:], op=ALU.mult), reads=[ev_b[e2], ld_b[igc]], writes=[ev_b[e2]])
            S.op("dve", lambda e, e1=e1, e2=e2, o=o: e.tensor_tensor(out=mt[:, o, :], in0=ev[e1][:], in1=ev[e2][:], op=ALU.add), reads=[ev_b[e1], ev_b[e2]], writes=[m_b[o]])
        if dbg is not None:
            for k in range(KC):
                ob = Buf("o"); outs.append(ob)
                S.dma("pool", dbg["ua"].rearrange("(k p) t -> p k t", p=P)[:, k, tok], ua[:, k, :], reads=[ua_b], writes=[ob])
                ob = Buf("o"); outs.append(ob)
                S.dma("pool", dbg["m"].rearrange("(k p) t -> p k t", p=P)[:, k, tok], mt[:, k, :], reads=[m_b[k]], writes=[ob])
            for k in range(KI):
                ob = Buf("o"); outs.append(ob)
                S.dma("pool", dbg["ygw"].rearrange("(k p) t -> p k t", p=P)[:, k, tok], ygw[:, k, :], reads=[ygw_b], writes=[ob])
            ob = Buf("o"); outs.append(ob)
            S.dma("sp", dbg["rs"][:, tok], rs_s[:], reads=[rs_b], writes=[ob])
        for o in range(KC):
            oc = slice(o * P, (o + 1) * P)
            wk = n["wo"] % 2; n["wo"] += 1
            S.dma("pool", wo[wk][:], wov[:, :, oc], writes=[wo_b[wk]])
            p3 = n["po"] % 2; n["po"] += 1
            S.mm_group([lambda e, k=k: e.matmul(po[p3][:], lhsT=wo[wk][:, k, :], rhs=mt[:, k, :], start=(k == 0), stop=(k == KC - 1)) for k in range(KC)],
                       reads=[wo_b[wk]] + m_b, writes=[po_b[p3]])
            ix = load(xin, o, tok)
            xi = n["xo"] % 2; n["xo"] += 1
            S.op("dve", lambda e, xi=xi, p3=p3, ix=ix, o=o: e.scalar_tensor_tensor(out=xo[xi][:], in0=po[p3][:], scalar=g2[:, o:o + 1], in1=ld[ix][:], op0=ALU.mult, op1=ALU.add),
                 reads=[po_b[p3], ld_b[ix], g2_b], writes=[xo_b[xi]])
            ob = Buf("o"); outs.append(ob)
            S.dma("sp", xout[:, o, tok], xo[xi][:], reads=[xo_b[xi]], writes=[ob])
    return cms, outs


P = 128


def emit_final(S, nc, xT_in, yT, gvec, T, D, eps=1e-6):
    KC = D // P
    cms = []

    def sb(name, shape, dt):
        cm = nc.sbuf_tensor(_uniq(name), shape, dt); t = cm.__enter__(); cms.append(cm); return t

    def ps(name, shape, dt=F32):
        cm = nc.psum_tensor(_uniq(name), shape, dt); t = cm.__enter__(); cms.append(cm); return t

    xin = xT_in.rearrange("(k p) t -> p k t", p=P); yout = yT.rearrange("(k p) t -> p k t", p=P)
    g = sb("fn_g", [P, KC], F32); g_b = Buf("g")
    S.dma("sp", g[:], gvec.rearrange("(k p) -> p k", p=P), writes=[g_b], allow_slow_non_contiguous=True)
    ones = sb("fn_ones", [P, P], BF16); on_b = Buf("ones")
    S.op("pool", lambda e: e.memset(ones[:], 1.0), writes=[on_b])
    xs = [sb("fn_x%d" % i, [P, KC, 512], F32) for i in range(2)]; xs_b = [Buf("x") for _ in range(2)]
    sq = [sb("fn_sq%d" % i, [P, 512], BF16) for i in range(2)]; sq_b = [Buf("sq") for _ in range(2)]
    rstd = sb("fn_rstd", [P, 512], F32); r_b = Buf("rstd")
    yo = [sb("fn_y%d" % i, [P, KC, 512], F32) for i in range(2)]; yo_b = [Buf("y") for _ in range(2)]
    ss = ps("fn_ss", [P, 512]); ss_b = Buf("ss", psum=True)
    outs = []; n = 0
    for s in range(T // 512):
        i = s % 2; tok = slice(s * 512, (s + 1) * 512)
        S.dma("sp", xs[i][:], xin[:, :, tok], writes=[xs_b[i]])
        for k in range(KC):
            j = n % 2; n += 1
            S.op("act", lambda e, j=j, k=k: e.activation(out=sq[j][:], in_=xs[i][:, k, :], func=AF.Square), reads=[xs_b[i]], writes=[sq_b[j]])
            S.op("pe", lambda e, j=j, k=k: e.matmul(ss[:], lhsT=ones[:], rhs=sq[j][:], start=(k == 0), stop=(k == KC - 1)), reads=[on_b, sq_b[j]], writes=[ss_b])
        S.op("act", lambda e: e.activation(out=rstd[:], in_=ss[:], func=AF.Sqrt, scale=1.0 / D, bias=eps), reads=[ss_b], writes=[r_b])
        S.op("dve", lambda e: e.reciprocal(out=rstd[:], in_=rstd[:]), reads=[r_b], writes=[r_b])
        for k in range(KC):
            S.op("dve", lambda e, k=k: e.scalar_tensor_tensor(out=yo[i][:, k, :], in0=xs[i][:, k, :], scalar=g[:, k:k + 1], in1=rstd[:], op0=ALU.mult, op1=ALU.mult),
                 reads=[xs_b[i], g_b, r_b], writes=[yo_b[i]])
        ob = Buf("o"); outs.append(ob)
        S.dma("sp", yout[:, :, tok], yo[i][:], reads=[yo_b[i]], writes=[ob])
    return cms, outs

D, DFF, DEPTH = 2048, 5632, 2
WEIGHTS = [("w_ada", [DEPTH, D, 9 * D]), ("b_ada", [DEPTH, 9 * D]), ("norm_ffn1", [DEPTH, D]), ("ffn1_w13", [DEPTH, D, 2 * DFF]), ("ffn1_w2", [DEPTH, DFF, D]),
           ("norm_mix", [DEPTH, D]), ("w_in", [DEPTH, D, 18496]), ("ssm_conv_w", [DEPTH, 4, 6144]), ("ssm_conv_b", [DEPTH, 6144]), ("dt_bias", [DEPTH, 64]),
           ("a_log", [DEPTH, 64]), ("d_skip", [DEPTH, 64]), ("ssm_norm_w", [DEPTH, 4096]), ("w_ssm_out", [DEPTH, 4096, D]), ("dw_w", [DEPTH, 31, D]),
           ("dw_b", [DEPTH, D]), ("conv_ln_g", [DEPTH, D]), ("conv_ln_b", [DEPTH, D]), ("w_pw2", [DEPTH, D, D]), ("b_pw2", [DEPTH, D]), ("w_o", [DEPTH, D, D]),
           ("norm_ffn2", [DEPTH, D]), ("ffn2_w13", [DEPTH, D, 2 * DFF]), ("ffn2_w2", [DEPTH, DFF, D]), ("final_norm", [D])]


def build_program(T, depth=DEPTH, TH=2048, TT=1024, upto=None):
    nc = bass.Bass("TRN2", target_bir_lowering=False)
    dr = lambda n, s: nc.dram_tensor(n, s, F32, kind="ExternalInput").ap()
    xT = dr("xT", [D, T]); c_col = dr("c_col", [D, 1]); cst = dr("cst", [128, 640])
    W = {n: dr(n, s) for n, s in WEIGHTS}
    yT = nc.dram_tensor("yT", [D, T], F32, kind="ExternalOutput").ap()
    scr = lambda n, r: nc.dram_tensor(n, [r, T], F32, kind="Internal").ap()
    xa, xb = scr("s_xa", D), scr("s_xb", D)
    sz, xbc, dtr, u, gat, xbcs, uc, y = scr("s_sz", 4096), scr("s_xbc", 6144), scr("s_dtr", 64), scr("s_u", 2048), scr("s_gat", 4096), scr("s_xbcs", 6144), scr("s_uc", 2048), scr("s_y", 4096)
    S = Sched(nc)
    cm_mod = nc.sbuf_tensor("mod", [128, 9, 16], F32); mod = cm_mod.__enter__(); mod_b = Buf("mod")

    def phase(r):
        cms = r[0] if isinstance(r, tuple) else r
        S.phase_end()
        for cm in reversed(cms):
            cm.__exit__(None, None, None)

    TH = min(TH, T); TT = min(TT, T)
    cur = xT
    stages = 0
    for l in range(depth):
        phase(emit_ada(S, nc, c_col, W["w_ada"][l], W["b_ada"][l], [W["norm_ffn1"][l], W["norm_mix"][l], W["norm_ffn2"][l]], D, mod, mod_b))
        nxt = xa if cur is not xa else xb
        phase(emit_ffn_s(S, nc, cur, nxt, W["ffn1_w13"][l], W["ffn1_w2"][l], mod[:, 0, :], mod[:, 1, :], mod[:, 2, :], mod_b, mod_b, mod_b, T, TT, D, DFF))
        cur = nxt
        for t0 in range(0, T, TH):
            ts = slice(t0, t0 + TH)

            def side31(S_, nc_, sb_, outs_, l=l, t0=t0):
                w31, w31b, b31, b31b = load_chan_params(S_, nc_, sb_, W["dw_w"][l], W["dw_b"][l], 2048, 31, "c31")
                yield from gen_dwconv(S_, nc_, sb_, u, uc, w31, w31b, b31, b31b, 2048, 31, T, None, "c31", outs_, TB=min(1024, TH), t_start=t0, t_end=t0 + TH)
            phase(emit_mix1(S, nc, cur[:, ts], W["w_in"][l], mod[:, 3, :], mod[:, 4, :], mod_b, mod_b, sz[:, ts], xbc[:, ts], dtr[:, ts], u[:, ts], gat[:, ts], TH, D,
                            side=side31))
        phase(emit_mix2(S, nc, xbc, u, xbcs, uc, W["ssm_conv_w"][l], W["ssm_conv_b"][l], W["dw_w"][l], W["dw_b"][l], T, do31=False))
        phase(emit_mix3(S, nc, xbcs, dtr, y, cst, W["dt_bias"][l], W["a_log"][l], W["d_skip"][l], T))
        nxt = xa if cur is not xa else xb
        phase(emit_mix4(S, nc, y, sz, uc, gat, cur, nxt, W["ssm_norm_w"][l], W["w_ssm_out"][l], W["conv_ln_g"][l], W["conv_ln_b"][l], W["w_pw2"][l], W["b_pw2"][l], W["w_o"][l],
                        mod[:, 5, :], mod_b, T, D))
        cur = nxt
        nxt = xa if cur is not xa else xb
        phase(emit_ffn_s(S, nc, cur, nxt, W["ffn2_w13"][l], W["ffn2_w2"][l], mod[:, 6, :], mod[:, 7, :], mod[:, 8, :], mod_b, mod_b, mod_b, T, TT, D, DFF))
        cur = nxt
    cms, outs = emit_final(S, nc, cur, yT, W["final_norm"], T, D)
    S.drain("sp")
    S.drain("act"); S.drain("dve"); S.drain("pe"); S.drain("pool")
    return nc, S


def kernel(**inputs):
    T = 4096
    nc, _ = build_program(T)
    x = np.asarray(inputs["x"], dtype=np.float32); c = np.asarray(inputs["c"], dtype=np.float32)
    cst = make_consts()
    wts = {n: np.ascontiguousarray(np.asarray(inputs[n], dtype=np.float32)) for n, _ in WEIGHTS}
    in_maps = []
    for core in range(8):
        b = core % 4
        m = {"xT": np.ascontiguousarray(x[b].T), "c_col": np.ascontiguousarray(c[b].reshape(D, 1)), "cst": cst}
        m.update(wts)
        in_maps.append(m)
    res = run_bass_kernel_spmd(nc, in_maps, core_ids=list(range(8)))
    return np.stack([np.ascontiguousarray(res.results[b]["yT"].T) for b in range(4)]).astype(np.float32)
```
